# Optimizing a Trainium2 kernel written in Bass

```python
import math
import jax, jax.numpy as jnp
from jax import lax
import numpy as np

D_MODEL = 1024
BATCH = 16
SEQ = 2048
DEPTH = 2

D_FF = 4 * D_MODEL
EPS = 1e-6
NEG_BIG = -1e30
M_HEADS = 4
M_DIM = D_MODEL // 16
M_WIDTH = M_HEADS * M_DIM
M_CONV = 4
M_CHUNK = 128
A_HEADS = 4
A_DIM = D_MODEL // 16
A_WIDTH = A_HEADS * 2 * A_DIM
ROT_DIM = A_DIM // 4
ROPE_THETA = 500000.0
Q_BLOCK = 128
H_HEADS = 4
H_DK = D_MODEL // 16
H_DV = D_MODEL // 16
H_WIDTH = H_HEADS * H_DV
H_CHUNK = 64
D_MIX = M_WIDTH + A_WIDTH + H_WIDTH
M_COLS = 4 * M_WIDTH + 2 * M_HEADS
A_COLS = 3 * A_WIDTH
H_COLS = 4 * H_WIDTH
D_IN = M_COLS + A_COLS + H_COLS

kernel_name = 'hymba_style_mlstm_diffattn_hgrn2'


def rmsnorm(x, g):
    xf = x.astype(jnp.float32)
    y = xf * lax.rsqrt(jnp.mean(xf * xf, axis=-1, keepdims=True) + EPS)
    return (y * g.astype(jnp.float32)).astype(x.dtype)


def causal_dwconv(x, w):
    K = w.shape[0]
    T = x.shape[1]
    xp = jnp.pad(x, ((0, 0), (K - 1, 0), (0, 0)))
    out = xp[:, 0:T] * w[0]
    for j in range(1, K):
        out = out + xp[:, j:j + T] * w[j]
    return out


def partial_rope(x, pos):
    half = ROT_DIM // 2
    inv = ROPE_THETA ** (-jnp.arange(half, dtype=jnp.float32) / half)
    ang = pos.astype(jnp.float32)[:, None] * inv[None, :]
    cos = jnp.cos(ang)[:, None, :]
    sin = jnp.sin(ang)[:, None, :]
    x1 = x[..., :half]
    x2 = x[..., half:ROT_DIM]
    r1 = x1 * cos - x2 * sin
    r2 = x2 * cos + x1 * sin
    return jnp.concatenate([r1, r2, x[..., ROT_DIM:]], axis=-1)


def to_chunks(a, L):
    B, H, T = a.shape[:3]
    a = a.reshape((B, H, T // L, L) + a.shape[3:])
    return jnp.moveaxis(a, 2, 0)


def from_chunks(a):
    a = jnp.moveaxis(a, 0, 2)
    B, H, NC, L = a.shape[:4]
    return a.reshape((B, H, NC * L) + a.shape[4:])


def mlstm_chunkwise(q, k, v, i_pre, f_pre):
    B, H, T, d = q.shape
    L = M_CHUNK
    k = k * (d ** -0.5)
    logf = jax.nn.log_sigmoid(f_pre)
    xs = (to_chunks(q, L), to_chunks(k, L), to_chunks(v, L), to_chunks(i_pre, L), to_chunks(logf, L))
    causal = jnp.tril(jnp.ones((L, L), dtype=bool))

    def step(carry, xs_c):
        C, n, m = carry
        qb, kb, vb, ib, fb = xs_c
        b = jnp.cumsum(fb, axis=-1)
        D = b[..., :, None] - b[..., None, :] + ib[..., None, :]
        D = jnp.where(causal, D, NEG_BIG)
        inter = b + m[..., None]
        m_t = jnp.maximum(jnp.max(D, axis=-1), inter)
        S = jnp.einsum('bhtd,bhsd->bhts', qb, kb) * jnp.exp(D - m_t[..., None])
        scale = jnp.exp(inter - m_t)
        num = jnp.einsum('bhts,bhsv->bhtv', S, vb) + scale[..., None] * jnp.einsum('bhvk,bhtk->bhtv', C, qb)
        den = jnp.sum(S, axis=-1) + scale * jnp.einsum('bhk,bhtk->bht', n, qb)
        h = num / jnp.maximum(jnp.abs(den), jnp.exp(-m_t))[..., None]
        bL = b[..., -1]
        g = bL[..., None] - b + ib
        m_new = jnp.maximum(bL + m, jnp.max(g, axis=-1))
        w_s = jnp.exp(g - m_new[..., None])
        decay = jnp.exp(bL + m - m_new)
        C_new = decay[..., None, None] * C + jnp.einsum('bhs,bhsv,bhsk->bhvk', w_s, vb, kb)
        n_new = decay[..., None] * n + jnp.einsum('bhs,bhsk->bhk', w_s, kb)
        return (C_new, n_new, m_new), h

    init = (jnp.zeros((B, H, d, d), jnp.float32), jnp.zeros((B, H, d), jnp.float32), jnp.zeros((B, H), jnp.float32))
    _, h = lax.scan(step, init, xs)
    return from_chunks(h)


def mlstm_mixer(q_pre, k_pre, v, o_pre, i_pre, f_pre, conv_w, gate_b, norm_g):
    B, T, _ = q_pre.shape
    f32 = jnp.float32
    qk = jax.nn.silu(causal_dwconv(jnp.concatenate([q_pre, k_pre], axis=-1).astype(f32), conv_w.astype(f32)))
    q, k = qk[..., :M_WIDTH], qk[..., M_WIDTH:]
    heads = lambda a: a.reshape(B, T, M_HEADS, M_DIM).transpose(0, 2, 1, 3)
    gb = gate_b.astype(f32)
    i_g = (i_pre.astype(f32) + gb[0]).transpose(0, 2, 1)
    f_g = (f_pre.astype(f32) + gb[1]).transpose(0, 2, 1)
    h = mlstm_chunkwise(heads(q), heads(k), heads(v.astype(f32)), i_g, f_g)
    h = rmsnorm(h.transpose(0, 2, 1, 3), norm_g)
    h = jax.nn.sigmoid(o_pre.astype(f32)).reshape(B, T, M_HEADS, M_DIM) * h
    return h.reshape(B, T, M_WIDTH)


def diff_attention(q, k, v, qn_g, kn_g, lam_vec, subln_g, lambda_init):
    B, T, _ = q.shape
    f32 = jnp.float32
    pos = jnp.arange(T)
    q = rmsnorm(q.astype(f32).reshape(B, T, A_HEADS, 2, A_DIM), qn_g)
    k = rmsnorm(k.astype(f32).reshape(B, T, A_HEADS, 2, A_DIM), kn_g)
    q = partial_rope(q.reshape(B, T, A_HEADS * 2, A_DIM), pos).reshape(B, T, A_HEADS, 2, A_DIM)
    k = partial_rope(k.reshape(B, T, A_HEADS * 2, A_DIM), pos).reshape(B, T, A_HEADS, 2, A_DIM)
    q = q * (A_DIM ** -0.5)
    v = v.astype(f32).reshape(B, T, A_HEADS, 2 * A_DIM).transpose(0, 2, 1, 3)
    kt = k.transpose(0, 2, 3, 1, 4)
    lv = lam_vec.astype(f32)
    lam = jnp.exp(jnp.sum(lv[0] * lv[1])) - jnp.exp(jnp.sum(lv[2] * lv[3])) + lambda_init
    NB = T // Q_BLOCK
    qb = q.reshape(B, NB, Q_BLOCK, A_HEADS, 2, A_DIM).transpose(1, 0, 3, 4, 2, 5)
    kpos = jnp.arange(T)

    def block(args):
        qi, bi = args
        s = jnp.einsum('bhcqd,bhckd->bhcqk', qi, kt)
        qpos = bi * Q_BLOCK + jnp.arange(Q_BLOCK)
        mask = qpos[:, None] >= kpos[None, :]
        p = jax.nn.softmax(jnp.where(mask, s, NEG_BIG), axis=-1)
        a = p[:, :, 0] - lam * p[:, :, 1]
        return jnp.einsum('bhqk,bhkv->bhqv', a, v)

    o = lax.map(block, (qb, jnp.arange(NB)))
    o = o.transpose(1, 0, 3, 2, 4).reshape(B, T, A_HEADS, 2 * A_DIM)
    o = rmsnorm(o, subln_g) * (1.0 - lambda_init)
    return o.reshape(B, T, A_WIDTH)


def hgrn2_chunkwise(q, logf, k, v):
    B, H, T, dk = q.shape
    dv = v.shape[-1]
    L = H_CHUNK
    xs = (to_chunks(q, L), to_chunks(logf, L), to_chunks(k, L), to_chunks(v, L))
    causal = jnp.tril(jnp.ones((L, L), dtype=bool))[..., None]

    def step(S, xs_c):
        qb, gb, kb, vb = xs_c
        Bc = jnp.cumsum(gb, axis=-2)
        diff = Bc[..., :, None, :] - Bc[..., None, :, :]
        decay = jnp.exp(jnp.where(causal, diff, NEG_BIG))
        A = jnp.einsum('bhtk,bhsk,bhtsk->bhts', qb, kb, decay)
        o = jnp.einsum('bhts,bhsv->bhtv', A, vb) + jnp.einsum('bhtk,bhkv->bhtv', qb * jnp.exp(Bc), S)
        BL = Bc[..., -1:, :]
        S_new = jnp.exp(BL[..., 0, :])[..., None] * S + jnp.einsum('bhsk,bhsv->bhkv', kb * jnp.exp(BL - Bc), vb)
        return S_new, o

    _, o = lax.scan(step, jnp.zeros((B, H, dk, dv), jnp.float32), xs)
    return from_chunks(o)


def hgrn2_mixer(q_pre, f_pre, i_pre, g_pre, lb, norm_g):
    B, T, _ = q_pre.shape
    f32 = jnp.float32
    heads = lambda a, dd: a.astype(f32).reshape(B, T, H_HEADS, dd).transpose(0, 2, 1, 3)
    q = jax.nn.silu(heads(q_pre, H_DK))
    fp = heads(f_pre, H_DK)
    lbh = lb.reshape(H_HEADS, 1, H_DK)
    f = lbh + (1.0 - lbh) * jax.nn.sigmoid(fp)
    logf = jnp.log(f)
    k = (1.0 - lbh) * jax.nn.sigmoid(-fp)
    v = heads(i_pre, H_DV)
    o = hgrn2_chunkwise(q, logf, k, v).transpose(0, 2, 1, 3)
    o = rmsnorm(o, norm_g) * jax.nn.silu(g_pre.astype(f32)).reshape(B, T, H_HEADS, H_DV)
    return o.reshape(B, T, H_WIDTH)


def setup_inputs(seed: int = 0) -> dict:
    key = jax.random.key(seed)
    ks = jax.random.split(key, 20)
    f32 = jnp.float32
    nrm = lambda k, shp, s: jax.random.normal(k, shp, f32) * s
    f_bias = jnp.broadcast_to(jnp.linspace(3.0, 6.0, M_HEADS, dtype=f32), (DEPTH, M_HEADS))
    mlstm_gate_b = jnp.stack([nrm(ks[4], (DEPTH, M_HEADS), 0.1), f_bias + nrm(ks[5], (DEPTH, M_HEADS), 0.1)], axis=1)
    return {
        'x': jax.random.normal(ks[0], (BATCH, SEQ, D_MODEL), f32),
        'norm1_g': 1.0 + nrm(ks[1], (DEPTH, D_MODEL), 0.02),
        'w_in': nrm(ks[2], (DEPTH, D_MODEL, D_IN), D_MODEL ** -0.5),
        'mlstm_conv_w': nrm(ks[3], (DEPTH, M_CONV, 2 * M_WIDTH), M_CONV ** -0.5),
        'mlstm_gate_b': mlstm_gate_b,
        'mlstm_norm_g': 1.0 + nrm(ks[6], (DEPTH, M_DIM), 0.02),
        'diff_q_norm_g': 1.0 + nrm(ks[7], (DEPTH, A_DIM), 0.02),
        'diff_k_norm_g': 1.0 + nrm(ks[8], (DEPTH, A_DIM), 0.02),
        'diff_lambda': nrm(ks[9], (DEPTH, 4, A_DIM), 0.1),
        'diff_subln_g': 1.0 + nrm(ks[10], (DEPTH, 2 * A_DIM), 0.02),
        'hgrn_lb_param': nrm(ks[11], (DEPTH, H_HEADS * H_DK), 1.0),
        'hgrn_norm_g': 1.0 + nrm(ks[12], (DEPTH, H_DV), 0.02),
        'w_out': nrm(ks[13], (DEPTH, D_MIX, D_MODEL), D_MIX ** -0.5),
        'norm2_g': 1.0 + nrm(ks[14], (DEPTH, D_MODEL), 0.02),
        'w_up': nrm(ks[15], (DEPTH, D_MODEL, D_FF), D_MODEL ** -0.5),
        'w_down': nrm(ks[16], (DEPTH, D_FF, D_MODEL), D_FF ** -0.5),
    }


def reference(x, norm1_g, w_in, mlstm_conv_w, mlstm_gate_b, mlstm_norm_g, diff_q_norm_g, diff_k_norm_g,
              diff_lambda, diff_subln_g, hgrn_lb_param, hgrn_norm_g, w_out, norm2_g, w_up, w_down):
    sm = jax.nn.softmax(hgrn_lb_param.astype(jnp.float32), axis=0)
    lower_bounds = jnp.cumsum(sm, axis=0) - sm[0]
    split_at = np.cumsum([M_WIDTH, M_WIDTH, M_WIDTH, M_WIDTH, M_HEADS, M_HEADS,
                          A_WIDTH, A_WIDTH, A_WIDTH,
                          H_WIDTH, H_WIDTH, H_WIDTH])
    for l in range(DEPTH):
        lambda_init = 0.8 - 0.6 * math.exp(-0.3 * l)
        h = rmsnorm(x, norm1_g[l])
        u = jnp.einsum('btd,de->bte', h, w_in[l])
        (mq, mk, mv, mo, mi, mf, aq, ak, av, hq, hf, hi, hg) = jnp.split(u, split_at, axis=-1)
        y_m = mlstm_mixer(mq, mk, mv, mo, mi, mf, mlstm_conv_w[l], mlstm_gate_b[l], mlstm_norm_g[l])
        y_a = diff_attention(aq, ak, av, diff_q_norm_g[l], diff_k_norm_g[l], diff_lambda[l], diff_subln_g[l], lambda_init)
        y_h = hgrn2_mixer(hq, hf, hi, hg, lower_bounds[l], hgrn_norm_g[l])
        mix = jnp.concatenate([y_m, y_a, y_h], axis=-1).astype(x.dtype)
        x = x + jnp.einsum('bte,ed->btd', mix, w_out[l])
        h2 = rmsnorm(x, norm2_g[l])
        a = jnp.square(jax.nn.relu(jnp.einsum('btd,df->btf', h2, w_up[l])))
        x = x + jnp.einsum('btf,fd->btd', a, w_down[l])
    return x
```

```python
import math
import numpy as np
from contextlib import ExitStack
import concourse.bass as bass
import concourse.mybir as mybir
from concourse.bass_utils import run_bass_kernel_spmd

F32 = mybir.dt.float32
BF16 = mybir.dt.bfloat16
AF = mybir.ActivationFunctionType
ALU = mybir.AluOpType
AX = mybir.AxisListType
D = 1024
DIN = 3592
DFF = 4096
EPS = 1e-6
ENGS = ("pe", "act", "dve", "pool", "sp")


class Buf:
    __slots__ = ("lw", "rd")

    def __init__(self):
        self.lw = None
        self.rd = {}


class Sched:
    def __init__(self, nc, stack):
        self.nc, self.stack = nc, stack
        self.prog = {e: [] for e in ENGS}
        self.sems, self.cnt, self.waited = {}, {}, {}
        for e in ENGS:
            self.new_sem(e)

    def new_sem(self, key):
        self.sems[key] = self.stack.enter_context(self.nc.semaphore("s_" + str(key)))
        self.cnt[key] = 0
        return key

    def _deps(self, reads, writes):
        deps = {}
        for b in reads:
            if b.lw is not None and deps.get(b.lw[0], 0) < b.lw[1]:
                deps[b.lw[0]] = b.lw[1]
        for b in writes:
            if b.lw is not None and deps.get(b.lw[0], 0) < b.lw[1]:
                deps[b.lw[0]] = b.lw[1]
            for k, v in b.rd.items():
                if deps.get(k, 0) < v:
                    deps[k] = v
        return deps

    def _waits(self, eng, deps):
        ws = []
        for k, v in deps.items():
            if self.waited.get((eng, k), 0) < v:
                self.waited[(eng, k)] = v
                ws.append((k, v))
                if k in self.prog:
                    self.prog[k][v - 1][3] = True
        return ws

    def op(self, eng, fn, reads=(), writes=()):
        deps = self._deps(reads, writes)
        if eng == "pe":
            deps.pop("pe", None)
        ws = self._waits(eng, deps)
        self.prog[eng].append([ws, fn, eng, False])
        n = len(self.prog[eng])
        for b in reads:
            b.rd[eng] = n
        for b in writes:
            b.lw = (eng, n)
            b.rd = {}

    def dma(self, queue, fns, semkey, reads=(), writes=()):
        if not isinstance(fns, (list, tuple)):
            fns = [fns]
        deps = self._deps(reads, writes)
        ws = self._waits(queue, deps)
        for j, fn in enumerate(fns):
            self.cnt[semkey] += 16
            self.prog[queue].append([ws if j == 0 else [], fn, semkey, True])
        n = self.cnt[semkey]
        for b in reads:
            b.rd[semkey] = n
        for b in writes:
            b.lw = (semkey, n)
            b.rd = {}

    def alias(self, new_bufs, old_bufs):
        m = {}
        for b in old_bufs:
            if b.lw is not None and m.get(b.lw[0], 0) < b.lw[1]:
                m[b.lw[0]] = b.lw[1]
            for k, v in b.rd.items():
                if m.get(k, 0) < v:
                    m[k] = v
        for b in new_bufs:
            b.lw = None
            b.rd = dict(m)

    def wait_all(self, eng, bufs):
        deps = self._deps(bufs, bufs)
        ws = self._waits(eng, deps)
        self.prog[eng].append([ws, None, None, False])

    def emit(self):
        nc = self.nc
        names = {"pe": "tensor", "act": "scalar", "dve": "vector", "pool": "gpsimd", "sp": "sync"}
        pref = {}
        for e in ENGS:
            c, arr = 0, []
            for ent in self.prog[e]:
                if ent[3] and ent[2] == e:
                    c += 1
                arr.append(c)
            pref[e] = arr
        sems = self.sems
        with nc.Block() as block:
            for e in ENGS:
                def body(eng, e=e):
                    for ws, fn, inckey, flagged in self.prog[e]:
                        for k, v in ws:
                            eng.wait_ge(sems[k], pref[k][v - 1] if k in pref else v)
                        if fn is not None:
                            ins = fn(eng)
                            if inckey == e:
                                if flagged:
                                    ins.then_inc(sems[e], 1)
                            else:
                                ins.then_inc(sems[inckey], 16)
                getattr(block, names[e])(body)


def host_consts(T):
    NT = T // 128
    c = np.zeros((128, 648), np.float32)
    i = np.arange(128)
    c[:, 0:128] = np.eye(128)
    c[:, 128:256] = (i[:, None] <= i[None, :])
    c[:, 256:384] = 1.0
    sl, tl = i[:, None] % 64, i[None, :] % 64
    same = (i[:, None] // 64) == (i[None, :] // 64)
    c[:, 384:512] = same * ((sl <= tl).astype(np.float32) - (sl <= 31))
    c[:, 512:640] = same * (sl <= tl)
    for a in range(2):
        c[:, 640 + 2 * a] = ((i // 64) == a) & ((i % 64) <= 31)
        c[:, 641 + 2 * a] = ((i // 64) == a) & ((i % 64) > 31)
    half = 8
    inv = (500000.0 ** (-np.arange(half, dtype=np.float32) / half)).astype(np.float32)
    ang = np.arange(T, dtype=np.float32)[:, None] * inv[None, :]
    cos, sin = np.cos(ang).astype(np.float32), np.sin(ang).astype(np.float32)
    tab = np.concatenate([cos, cos, -sin, sin], axis=1)
    rope = np.ascontiguousarray(tab.reshape(NT, 128, 32).transpose(1, 0, 2))
    return c, rope


def build(T=2048, NSEQ=2, DEPTH=2, dbg=False, stop=99, sub=0):
    NT = T // 128
    NG = T // 512
    nc = bass.Bass("TRN2", target_bir_lowering=False)
    dt_in = lambda name, shape: nc.dram_tensor(name, shape, F32, kind="ExternalInput").ap()
    x_d = dt_in("x", [NSEQ * T, D])
    w_in = dt_in("w_in", [DEPTH, D, DIN])
    w_out = dt_in("w_out", [DEPTH, D, D])
    w_up = dt_in("w_up", [DEPTH, D, DFF])
    w_dn = dt_in("w_down", [DEPTH, DFF, D])
    n1g = dt_in("norm1_g", [DEPTH, D])
    n2g = dt_in("norm2_g", [DEPTH, D])
    convw = dt_in("mlstm_conv_w", [DEPTH, 4, 512])
    gateb = dt_in("mlstm_gate_b", [DEPTH, 2, 4])
    mng = dt_in("mlstm_norm_g", [DEPTH, 64])
    qng = dt_in("diff_q_norm_g", [DEPTH, 64])
    kng = dt_in("diff_k_norm_g", [DEPTH, 64])
    dlam = dt_in("diff_lambda", [DEPTH, 4, 64])
    slg = dt_in("diff_subln_g", [DEPTH, 128])
    lbp = dt_in("hgrn_lb_param", [DEPTH, 256])
    hng = dt_in("hgrn_norm_g", [DEPTH, 64])
    cst_d = dt_in("cst", [128, 648])
    rope_d = dt_in("rope", [128, NT, 32])
    y_d = nc.dram_tensor("y", [NSEQ * T, D], F32, kind="ExternalOutput").ap()
    if dbg:
        dbg_d = nc.dram_tensor("dbg", [128, 8, T], BF16, kind="ExternalOutput").ap()

    with ExitStack() as st:
        S = Sched(nc, st)
        off = [16512]

        def sb(name, shape, dt, at=None):
            n = 1
            for s_ in shape[1:]:
                n *= s_
            nbytes = n * (2 if dt == BF16 else 4)
            nbytes = (nbytes + 31) // 32 * 32
            if at is None:
                at = off[0]
                off[0] += nbytes
            return nc.alloc_sbuf_tensor_at(name, list(shape), dt, offset=at)

        X = sb("X", [128, NT, D], F32)
        Xb = [Buf() for _ in range(NT)]
        HT = sb("HT", [128, 8, T], BF16)
        HTb = [Buf() for _ in range(NT)]
        MIXT = sb("MIXT", [128, 8, T], BF16)
        MXb = [Buf() for _ in range(NT)]
        NS = 3
        RING = [sb("ring%d" % i, [128, 4096], BF16) for i in range(NS)]
        RGb = [Buf() for _ in range(NS)]
        for i in range(NS):
            S.new_sem("w%d" % i)
        CST = sb("CST", [128, 648], F32); CSTb = Buf()
        ROPE = sb("ROPE", [128, NT, 32], F32); ROPEb = Buf()
        IDB = sb("IDB", [128, 128], BF16)
        MSKB = sb("MSKB", [128, 128], BF16)
        MSKHB = sb("MSKHB", [128, 128], BF16)
        ONESB = sb("ONESB", [128, 128], BF16)
        TSHB = sb("TSHB", [128, 128], BF16)
        SELB = sb("SELB", [128, 4], BF16)
        CBb = Buf()
        EPSt = sb("EPSt", [128, 1], F32)
        ONEt = sb("ONEt", [128, 1], F32)
        G1C = sb("G1C", [128, DEPTH, 8], F32)
        G2C = sb("G2C", [128, DEPTH, 8], F32)
        CW = sb("CW", [128, DEPTH, 4, 4], F32)
        GBB = sb("GBB", [128, DEPTH, 8], F32)
        MNG = sb("MNG", [128, DEPTH, 64], F32)
        GQK = sb("GQK", [128, DEPTH, 4, 64], F32)
        SLG = sb("SLG", [128, DEPTH, 128], F32)
        HNG = sb("HNG", [128, DEPTH, 64], F32)
        LBP = sb("LBP", [128, DEPTH, 256], F32)
        LBB = sb("LBB", [128, DEPTH, 256], F32)
        OMLB = sb("OMLB", [128, DEPTH, 256], F32)
        LAMI = sb("LAMI", [128, DEPTH, 4, 64], F32)
        NLAM = sb("NLAM", [128, DEPTH, 2], F32)
        WG = sb("WG", [128, DEPTH, 8, 8], BF16)
        PRMb = Buf()
        S.new_sem("ldc"); S.new_sem("ldr"); S.new_sem("ldp"); S.new_sem("ldg"); S.new_sem("std")
        for i in range(NT):
            S.new_sem("ldx%d" % i); S.new_sem("sty%d" % i)
        WGb = Buf()
        pfns, gfns = [], []
        PW0 = off[0]
        PWEND = 229344
        assert PW0 + 36000 <= PWEND, (PW0, PWEND)

        PS = [nc.alloc_psum_tensor("ps%d" % i, [128, 512], F32) for i in range(8)]
        PSb = [Buf() for _ in range(8)]

        def pbf(i):
            return PS[i][:].bitcast(BF16)

        def ACT(out, in_, func, reads, writes, bias=None, scale=None, accum=None):
            kw = {}
            if bias is not None:
                kw["bias"] = bias
            if scale is not None:
                kw["scale"] = scale
            if accum is not None:
                kw["accum_out"] = accum
            S.op("act", lambda e: e.activation(out=out, in_=in_, func=func, **kw), reads, writes)

        def TT(out, in0, in1, op, reads, writes, eng="dve"):
            S.op(eng, lambda e: e.tensor_tensor(out=out, in0=in0, in1=in1, op=op), reads, writes)

        def TS(out, in0, s1, op0, reads, writes, s2=None, op1=None, eng="dve"):
            if op1 is None:
                S.op(eng, lambda e: e.tensor_scalar(out=out, in0=in0, scalar1=s1, scalar2=None, op0=op0), reads, writes)
            else:
                S.op(eng, lambda e: e.tensor_scalar(out=out, in0=in0, scalar1=s1, scalar2=s2, op0=op0, op1=op1), reads, writes)

        def STT(out, in0, scalar, in1, op0, op1, reads, writes):
            S.op("dve", lambda e: e.scalar_tensor_tensor(out=out, in0=in0, scalar=scalar, in1=in1, op0=op0, op1=op1),
                 reads, writes)

        def CP(out, in_, reads, writes, eng="dve"):
            if eng == "act":
                S.op("act", lambda e: e.copy(out=out, in_=in_), reads, writes)
            else:
                S.op(eng, lambda e: e.tensor_copy(out=out, in_=in_), reads, writes)

        def RECIP(out, in_, reads, writes):
            S.op("dve", lambda e: e.reciprocal(out=out, in_=in_), reads, writes)

        def RED(out, in_, reads, writes):
            S.op("dve", lambda e: e.tensor_reduce(out=out, in_=in_, axis=AX.X, op=ALU.add), reads, writes)

        def MM(out, lhsT, rhs, start, stop, reads, writes):
            S.op("pe", lambda e: e.matmul(out, lhsT=lhsT, rhs=rhs, start=start, stop=stop), reads, writes)

        def TR(out, in_, reads, writes):
            S.op("pe", lambda e: e.transpose(out=out, in_=in_, identity=IDB[:]), list(reads) + [CBb], writes)

        def RSTD(out, in_, scale, reads, writes):
            P = out.shape[0]
            ACT(out, in_, AF.Sqrt, reads, writes, bias=EPSt[0:P, :], scale=scale)
            RECIP(out, out, writes, writes)

        wctr = [0]

        def WLOAD(parts):
            s = wctr[0] % NS
            wctr[0] += 1
            fns = []
            for dst, src in parts:
                for k in range(8):
                    fns.append(lambda e, dst=dst, src=src, k=k: e.dma_start(out=dst(RING[s])[:, k, :], in_=src[:, k, :]))
            S.dma("pool", fns, "w%d" % s, writes=[RGb[s]])
            return RING[s], RGb[s]

        S.dma("sp", lambda e: e.dma_start(out=CST[:], in_=cst_d[:, :]), "ldc", writes=[CSTb])
        S.dma("sp", lambda e: e.dma_start(out=ROPE[:], in_=rope_d[:, :, :]), "ldr", writes=[ROPEb])
        CP(IDB[:], CST[:, 0:128], [CSTb], [CBb])
        CP(MSKB[:], CST[:, 128:256], [CSTb], [CBb])
        CP(MSKHB[:], CST[:, 512:640], [CSTb], [CBb])
        CP(ONESB[:], CST[:, 256:384], [CSTb], [CBb])
        CP(TSHB[:], CST[:, 384:512], [CSTb], [CBb])
        CP(SELB[:], CST[:, 640:644], [CSTb], [CBb])
        S.op("dve", lambda e: e.memset(EPSt[:], EPS), [], [CBb])
        S.op("dve", lambda e: e.memset(ONEt[:], 1.0), [], [CBb])
        for l in range(DEPTH):
            for k in range(8):
                for (dst, src) in ((G1C, n1g), (G2C, n2g)):
                    pfns.append(lambda e, dst=dst, src=src, l=l, k=k: e.dma_start(
                        out=dst[:, l, k:k + 1], in_=src[l:l + 1, k * 128:(k + 1) * 128].rearrange("o p -> p o")))
            for ct in range(4):
                for j in range(4):
                    pfns.append(lambda e, l=l, ct=ct, j=j: e.dma_start(
                        out=CW[:, l, ct, j:j + 1],
                        in_=convw[l, j:j + 1, ct * 128:(ct + 1) * 128].rearrange("o p -> p o")))
            bc = lambda ap, n: ap.to_broadcast([128, n])
            pfns.append(lambda e, l=l: e.dma_start(out=GBB[:, l, :], in_=bc(gateb[l:l + 1].rearrange("o a b -> o (a b)"), 8)))
            pfns.append(lambda e, l=l: e.dma_start(out=MNG[:, l, :], in_=bc(mng[l:l + 1, :], 64)))
            for j in range(2):
                pfns.append(lambda e, l=l, j=j: e.dma_start(out=GQK[:, l, j, :], in_=bc(qng[l:l + 1, :], 64)))
                pfns.append(lambda e, l=l, j=j: e.dma_start(out=GQK[:, l, 2 + j, :], in_=bc(kng[l:l + 1, :], 64)))
            pfns.append(lambda e, l=l: e.dma_start(out=SLG[:, l, :], in_=bc(slg[l:l + 1, :], 128)))
            pfns.append(lambda e, l=l: e.dma_start(out=HNG[:, l, :], in_=bc(hng[l:l + 1, :], 64)))
            pfns.append(lambda e, l=l: e.dma_start(out=LAMI[:, l, :, :].rearrange("p a b -> p (a b)"),
                                                   in_=bc(dlam[l:l + 1].rearrange("o a b -> o (a b)"), 256)))
            pfns.append(lambda e, l=l: e.dma_start(out=LBP[:, l, :], in_=bc(lbp[l:l + 1, :], 256)))
            for k in range(8):
                gfns.append(lambda e, l=l, k=k: e.dma_start(
                    out=WG[:, l, k, :], in_=w_in[l, k * 128:(k + 1) * 128, 1024:1032]))
        S.dma("sp", pfns, "ldp", writes=[PRMb])
        S.dma("pool", gfns, "ldg", writes=[WGb])
        lam_init = [0.8 - 0.6 * math.exp(-0.3 * l) for l in range(DEPTH)]
        PT0 = sb("PT0", [128, DEPTH, 256], F32, at=PW0)
        PT1 = sb("PT1", [128, 256], F32, at=PW0 + DEPTH * 1024)
        PT2 = sb("PT2", [128, 8], F32, at=PW0 + DEPTH * 1024 + 1024)
        PTb = Buf()
        pw_bufs = [PTb]
        ACT(PT0[:], LBP[:], AF.Exp, [PRMb], [PTb])
        CP(PT1[:], PT0[:, 0, :], [PTb], [PTb])
        for l in range(1, DEPTH):
            TT(PT1[:], PT1[:], PT0[:, l, :], ALU.add, [PTb], [PTb])
        RECIP(PT1[:], PT1[:], [PTb], [PTb])
        for l in range(DEPTH):
            TT(PT0[:, l, :], PT0[:, l, :], PT1[:], ALU.mult, [PTb], [PTb])
        CP(LBB[:, 0, :], PT0[:, 0, :], [PTb], [PRMb])
        for l in range(1, DEPTH):
            TT(LBB[:, l, :], LBB[:, l - 1, :], PT0[:, l, :], ALU.add, [PTb, PRMb], [PRMb])
        for l in range(DEPTH):
            TT(LBB[:, l, :], LBB[:, l, :], PT0[:, 0, :], ALU.subtract, [PTb, PRMb], [PRMb])
            TS(OMLB[:, l, :], LBB[:, l, :], -1.0, ALU.mult, [PRMb], [PRMb], s2=1.0, op1=ALU.add)
        for l in range(DEPTH):
            TT(PT1[:, 0:64], LAMI[:, l, 0, :], LAMI[:, l, 1, :], ALU.mult, [PRMb, PTb], [PTb])
            TT(PT1[:, 64:128], LAMI[:, l, 2, :], LAMI[:, l, 3, :], ALU.mult, [PRMb, PTb], [PTb])
            RED(PT2[:, 0:2], PT1[:, 0:128].rearrange("p (a b) -> p a b", a=2), [PTb], [PTb])
            ACT(PT2[:, 0:2], PT2[:, 0:2], AF.Exp, [PTb], [PTb])
            TT(PT2[:, 2:3], PT2[:, 1:2], PT2[:, 0:1], ALU.subtract, [PTb], [PTb])
            TS(NLAM[:, l, 0:1], PT2[:, 2:3], -lam_init[l], ALU.add, [PTb, PRMb], [PRMb])
            TS(GQK[:, l, 0:2, :], GQK[:, l, 0:2, :], 0.125, ALU.mult, [PRMb], [PRMb])
            TS(SLG[:, l, :], SLG[:, l, :], 1.0 - lam_init[l], ALU.mult, [PRMb], [PRMb])

        def new_phase(n):
            bs = [Buf() for _ in range(n)]
            S.alias(bs, pw_bufs)
            pw_bufs[:] = bs
            return bs

        def norm_phase(GC, l):
            bs = new_phase(5)
            bj, bss, bxs0, bxs1, brs = bs
            JUNK = sb("JUNK", [128, D], BF16, at=PW0)
            XS = [sb("XS%d" % i, [128, D], BF16, at=PW0 + 2048 + 2048 * i) for i in range(2)]
            SSQ = sb("SSQ", [128, NT], F32, at=PW0 + 6144)
            RS = sb("RS", [128, NT], F32, at=PW0 + 6144 + 128)
            bxs = [bxs0, bxs1]
            for i in range(NT):
                ACT(JUNK[:], X[:, i, :], AF.Square, [Xb[i]], [bj, bss], accum=SSQ[:, i:i + 1])
            ACT(RS[:], SSQ[:], AF.Sqrt, [bss, CBb], [brs], bias=EPSt[:], scale=1.0 / D)
            RECIP(RS[:], RS[:], [brs], [brs])
            for i in range(NT):
                p = i % 2
                ACT(XS[p][:], X[:, i, :], AF.Copy, [Xb[i], brs], [bxs[p]], scale=RS[:, i:i + 1])
                pt = pbf(p)
                for k in range(8):
                    TR(pt[:, k * 128:(k + 1) * 128], XS[p][:, k * 128:(k + 1) * 128], [bxs[p]], [PSb[p]])
                TT(HT[:, :, i * 128:(i + 1) * 128], pt.rearrange("p (k t) -> p k t", k=8),
                   GC[:, l, :].unsqueeze(2).to_broadcast([128, 8, 128]), ALU.mult, [PSb[p], PRMb], [HTb[i]])

        def wsrc_in(l, c0, n):
            return w_in[l, :, c0:c0 + n].rearrange("(k p) n -> p k n", p=128)

        def wview(ncols):
            return lambda t: t[:, 0:8 * ncols].rearrange("p (k n) -> p k n", k=8)

        def mlstm_phase(l):
            bs = new_phase(12)
            bqk = bs[0:4]
            bpre, bacc, bsg, bhalo = bs[4:6], bs[6:8], bs[8], bs[9]
            QKT = sb("QKT", [128, 4, T], BF16, at=PW0)
            o = PW0 + 8 * T
            PRE = [sb("PRE%d" % i, [128, 515], F32, at=o + 2080 * i) for i in range(2)]
            ACC = [sb("ACC%d" % i, [128, 512], F32, at=o + 4160 + 2048 * i) for i in range(2)]
            SG = sb("SGm", [128, 512], F32, at=o + 8256)
            HALO = sb("HALO", [128, 4, 3], F32, at=o + 10304)
            Wqk, bw = WLOAD([(wview(512), wsrc_in(l, 0, 512))])
            Wqv = wview(512)(Wqk)
            S.op("dve", lambda e: e.memset(HALO[:], 0.0), [], [bhalo])
            it = 0
            for G in range(NG):
                for ct in range(4):
                    p = it % 2
                    it += 1
                    pb_ = 6 + p
                    for k in range(8):
                        MM(PS[pb_][:], Wqv[:, k, ct * 128:(ct + 1) * 128], HT[:, k, G * 512:(G + 1) * 512],
                           k == 0, k == 7, [bw] + HTb[4 * G:4 * G + 4], [PSb[pb_]])
                    CP(PRE[p][:, 0:3], HALO[:, ct, :], [bhalo], [bpre[p]])
                    CP(PRE[p][:, 3:515], PS[pb_][:], [PSb[pb_]], [bpre[p]], eng="act")
                    CP(HALO[:, ct, :], PRE[p][:, 512:515], [bpre[p]], [bhalo])
                    TS(ACC[p][:], PRE[p][:, 0:512], CW[:, l, ct, 0:1], ALU.mult, [bpre[p], PRMb], [bacc[p]])
                    for j in range(1, 4):
                        STT(ACC[p][:], PRE[p][:, j:j + 512], CW[:, l, ct, j:j + 1], ACC[p][:], ALU.mult, ALU.add,
                            [bpre[p], PRMb, bacc[p]], [bacc[p]])
                    if ct < 2:
                        ACT(QKT[:, ct, G * 512:(G + 1) * 512], ACC[p][:], AF.Silu, [bacc[p]], [bqk[ct]])
                    else:
                        ACT(SG[:], ACC[p][:], AF.Sigmoid, [bacc[p]], [bsg])
                        STT(QKT[:, ct, G * 512:(G + 1) * 512], ACC[p][:], 0.125, SG[:], ALU.mult, ALU.mult,
                            [bacc[p], bsg], [bqk[ct]])
            if sub == 1:
                return
            old = [bpre[0], bpre[1], bacc[0], bacc[1], bsg, bhalo]
            nb = [Buf() for _ in range(16)]
            S.alias(nb, old)
            pw_bufs.extend(nb)
            bsm, bvbe, bkt, bsmt = nb[0:2], nb[2:4], nb[4:6], nb[6:8]
            bct, bctb, btc, bhh, bog, byb = nb[8], nb[9], nb[10], nb[11], nb[12], nb[13:15]
            SM = [sb("SMm%d" % i, [128, 64], F32, at=o + 256 * i) for i in range(2)]
            VBE = [sb("VBE%d" % i, [128, 4, 65], BF16, at=o + 512 + 544 * i) for i in range(2)]
            KTK = [sb("KTK%d" % i, [128, 256], BF16, at=o + 1600 + 512 * i) for i in range(2)]
            SMT = [sb("SMT%d" % i, [128, 4, 128], BF16, at=o + 2624 + 1024 * i) for i in range(2)]
            CTF = sb("CTF", [128, 2, 65], F32, at=o + 4672)
            CTB = sb("CTB", [128, 2, 65], BF16, at=o + 5216)
            TC = sb("TCm", [128, 2, 65], F32, at=o + 5504)
            HH = sb("HHm", [128, 256], F32, at=o + 6048)
            SQ = sb("SQm", [128, 256], F32, at=o + 7072)
            OG = sb("OGm", [128, 256], F32, at=o + 8096)
            YB = [sb("YBm%d" % i, [128, 256], BF16, at=o + 9120 + 512 * i) for i in range(2)]
            HLm = [sb("HLm%d" % i, [128, 8], BF16, at=o + 10144 + 32 * i) for i in range(2)]
            QZ = [sb("QZm%d" % i, [128, 4, 128], BF16, at=o + 10240 + 1024 * i) for i in range(2)]
            bqz = [Buf(), Buf()]
            S.alias(bqz, old)
            pw_bufs.extend(bqz)
            for i_ in range(2):
                S.op("dve", lambda e, i_=i_: e.memset(QZ[i_][:], 0.0), [], [bqz[i_]])
            bhl = [Buf(), Buf()]
            S.alias(bhl, old)
            pw_bufs.extend(bhl)
            Wvo, bwv = WLOAD([(wview(512), wsrc_in(l, 512, 512))])
            Wvv = wview(512)(Wvo)
            S.op("dve", lambda e: e.memset(CTF[:], 0.0), [], [bct])
            S.op("dve", lambda e: e.memset(CTB[:], 0.0), [], [bctb])
            for c in range(NT):
                p = c % 2
                cs = slice(c * 128, (c + 1) * 128)
                pvo = 0 if p == 0 else 5
                sm = SM[p]
                for k in range(8):
                    MM(PS[pvo][:], HT[:, k, cs], Wvv[:, k, :], k == 0, k == 7, [HTb[c], bwv], [PSb[pvo]])
                for k in range(8):
                    MM(PS[1][:, 0:8], HT[:, k, cs], WG[:, l, k, :], k == 0, k == 7, [HTb[c], WGb], [PSb[1]])
                TT(sm[:, 0:8], PS[1][:, 0:8], GBB[:, l, :], ALU.add, [PSb[1], PRMb], [bsm[p]])
                ACT(sm[:, 8:12], sm[:, 4:8], AF.Exp, [bsm[p]], [bsm[p]], scale=-1.0)
                ACT(sm[:, 8:12], sm[:, 8:12], AF.Ln, [bsm[p], CBb], [bsm[p]], bias=ONEt[:])
                hl_ = HLm[p]
                CP(hl_[:, 0:4], sm[:, 8:12], [bsm[p]], [bhl[p]])
                TT(sm[:, 48:52], sm[:, 8:12], hl_[:, 0:4], ALU.subtract, [bsm[p], bhl[p]], [bsm[p]])
                CP(hl_[:, 4:8], sm[:, 48:52], [bsm[p]], [bhl[p]])
                for q_ in range(2):
                    MM(PS[1][:, 8:12], MSKB[:], hl_[:, 4 * q_:4 * q_ + 4], q_ == 0, q_ == 1, [CBb, bhl[p]], [PSb[1]])
                for q_ in range(2):
                    MM(PS[1][:, 12:16], ONESB[:], hl_[:, 4 * q_:4 * q_ + 4], q_ == 0, q_ == 1, [CBb, bhl[p]], [PSb[1]])
                TT(sm[:, 12:16], sm[:, 0:4], PS[1][:, 8:12], ALU.add, [bsm[p], PSb[1]], [bsm[p]])
                ACT(sm[:, 16:20], sm[:, 12:16], AF.Exp, [bsm[p]], [bsm[p]])
                ACT(sm[:, 20:28], PS[1][:, 8:16], AF.Exp, [PSb[1]], [bsm[p]], scale=-1.0)
                CP(sm[0:64, 28:30], sm[0:64, 24:28:2], [bsm[p]], [bsm[p]])
                CP(sm[64:128, 28:30], sm[64:128, 25:28:2], [bsm[p]], [bsm[p]])
                TT(VBE[p][:, :, 0:64], PS[pvo][:, 0:256].rearrange("p (h d) -> p h d", h=4),
                   sm[:, 16:20].unsqueeze(2).to_broadcast([128, 4, 64]), ALU.mult, [PSb[pvo], bsm[p]], [bvbe[p]])
                CP(VBE[p][:, :, 64:65], sm[:, 16:20].unsqueeze(2), [bsm[p]], [bvbe[p]])
                ACT(OG[:], PS[pvo][:, 256:512], AF.Sigmoid, [PSb[pvo]], [bog])
                TT(OG[:].rearrange("p (h d) -> p h d", h=4), OG[:].rearrange("p (h d) -> p h d", h=4),
                   MNG[:, l, :].unsqueeze(1).to_broadcast([128, 4, 64]), ALU.mult, [bog, PRMb], [bog])
                pk = pbf(2)
                for j in range(2):
                    TR(pk[:, j * 128:(j + 1) * 128], QKT[:, 2 + j, cs], [bqk[2 + j]], [PSb[2]])
                CP(KTK[p][:], pk[:, 0:256], [PSb[2]], [bkt[p]], eng="act")
                CP(QZ[p][0:64, 0:4:2, :], QKT[0:64, 0:2, cs], [bqk[0], bqk[1]], [bqz[p]])
                CP(QZ[p][64:128, 1:4:2, :], QKT[64:128, 0:2, cs], [bqk[0], bqk[1]], [bqz[p]])
                for h in range(4):
                    MM(PS[3][:, h * 128:(h + 1) * 128], QKT[:, 2 + h // 2, cs], QZ[p][:, h, :], True, True,
                       [bqk[2 + h // 2], bqz[p]], [PSb[3]])
                TT(SMT[p][:], PS[3][:].rearrange("p (h t) -> p h t", h=4),
                   MSKB[:].unsqueeze(1).to_broadcast([128, 4, 128]), ALU.mult, [PSb[3], CBb], [bsmt[p]])
                pn = PS[4][:, 0:260].rearrange("p (h e) -> p h e", h=4)
                for h in range(4):
                    hs = slice((h % 2) * 64, (h % 2) * 64 + 64)
                    MM(pn[:, h, :], SMT[p][:, h, :], VBE[p][:, h, :], True, False, [bsmt[p], bvbe[p]], [PSb[4]])
                    MM(pn[:, h, :], QZ[p][:, h, :], CTB[:, h // 2, :], False, True, [bqz[p], bctb], [PSb[4]])
                pc = PS[2][:, 128:388].rearrange("p (a e) -> p a e", a=2)
                for hp in range(2):
                    MM(pc[:, hp, :], KTK[p][:, hp * 128:(hp + 1) * 128],
                       VBE[p][:, 2 * hp:2 * hp + 2, :].rearrange("p a e -> p (a e)"), True, True,
                       [bkt[p], bvbe[p]], [PSb[2]])
                ebl = sm[:, 28:30].unsqueeze(2).to_broadcast([128, 2, 65])
                for hl in range(2):
                    hs = slice(hl * 64, hl * 64 + 64)
                    TT(TC[hs, :, :], pc[hs, :, hl * 65:(hl + 1) * 65], sm[hs, 28:30].unsqueeze(2).to_broadcast([64, 2, 65]),
                       ALU.mult, [PSb[2], bsm[p]], [btc])
                TT(CTF[:], CTF[:], ebl, ALU.mult, [bct, bsm[p]], [bct])
                TT(CTF[:], CTF[:], TC[:], ALU.add, [bct, btc], [bct])
                TT(sm[:, 32:36], pn[:, :, 64:65].rearrange("p h e -> p (h e)"), sm[:, 20:24], ALU.mult, [PSb[4], bsm[p]], [bsm[p]])
                TS(sm[:, 44:48], sm[:, 32:36], -1.0, ALU.mult, [bsm[p]], [bsm[p]], s2=1.0, op1=ALU.max)
                TT(sm[:, 32:36], sm[:, 32:36], sm[:, 44:48], ALU.max, [bsm[p]], [bsm[p]])
                RECIP(sm[:, 32:36], sm[:, 32:36], [bsm[p]], [bsm[p]])
                TT(sm[:, 36:40], sm[:, 32:36], sm[:, 20:24], ALU.mult, [bsm[p]], [bsm[p]])
                TT(HH[:].rearrange("p (h d) -> p h d", h=4), pn[:, :, 0:64],
                   sm[:, 36:40].unsqueeze(2).to_broadcast([128, 4, 64]), ALU.mult, [PSb[4], bsm[p]], [bhh])
                CP(CTB[:], CTF[:], [bct], [bctb])
                ACT(SQ[:], HH[:], AF.Square, [bhh], [bhh])
                RED(sm[:, 40:44], SQ[:].rearrange("p (h d) -> p h d", h=4), [bhh], [bsm[p]])
                RSTD(sm[:, 40:44], sm[:, 40:44], 1.0 / 64, [bsm[p], CBb], [bsm[p]])
                TT(HH[:].rearrange("p (h d) -> p h d", h=4), HH[:].rearrange("p (h d) -> p h d", h=4),
                   sm[:, 40:44].unsqueeze(2).to_broadcast([128, 4, 64]), ALU.mult, [bhh, bsm[p]], [bhh])
                TT(YB[p][:], HH[:], OG[:], ALU.mult, [bhh, bog], [byb[p]])
                py = pbf(3 if False else 6)
                for j in range(2):
                    TR(py[:, j * 128:(j + 1) * 128], YB[p][:, j * 128:(j + 1) * 128], [byb[p]], [PSb[6]])
                CP(MIXT[:, 0:2, cs], py[:, 0:256].rearrange("p (j t) -> p j t", j=2), [PSb[6]], [MXb[c]], eng="act")

        def attn_phase(l):
            for h in range(4):
                bs = new_phase(20)
                bqkt, bvx, bsq, bss, bqn, bt1, bt2 = bs[0], bs[1], bs[2], bs[3], bs[4], bs[5], bs[6]
                bqb, bpt, bsmall, bto, bo, bya = bs[7:9], bs[9:12], bs[12:14], bs[14:16], bs[16:18], bs[18:20]
                QKT = sb("AQKT", [128, 2, T], BF16, at=PW0)
                o = PW0 + 4 * T
                VX = sb("AVX", [128, NT, 129], BF16, at=o)
                o += (NT * 258 + 31) // 32 * 32
                SQ = sb("ASQ", [128, 256], F32, at=o)
                QN = sb("AQN", [128, 4, 64], F32, at=o + 1024)
                T1 = sb("AT1", [128, 4, 16], F32, at=o + 2048)
                T2 = sb("AT2", [128, 4, 16], F32, at=o + 2304)
                SS = sb("ASS", [128, 8], F32, at=o + 2560)
                QB = [sb("AQB%d" % i, [128, 256], BF16, at=o + 2592 + 512 * i) for i in range(2)]
                PT = [sb("APT%d" % i, [128, 4, 128], BF16, at=o + 3616 + 1024 * i) for i in range(3)]
                SML = [sb("ASM%d" % i, [128, 8], F32, at=o + 6688 + 32 * i) for i in range(2)]
                TO = [sb("ATO%d" % i, [128, 128], F32, at=o + 6752 + 512 * i) for i in range(2)]
                OO = [sb("AOO%d" % i, [128, 128], F32, at=o + 7776 + 512 * i) for i in range(2)]
                YA = [sb("AYA%d" % i, [128, 128], BF16, at=o + 8800 + 256 * i) for i in range(2)]
                JK = sb("AJK", [128, 128], F32, at=o + 9312)
                QZ = sb("AQZ", [128, 2, T], BF16, at=o + 9824)
                bqz = Buf()
                S.alias([bqz], bs)
                pw_bufs.append(bqz)
                S.op("dve", lambda e, QZ=QZ: e.memset(QZ[:], 0.0), [], [bqz])
                Wa, bwa = WLOAD([(lambda t, j=j: t[:, 0:3072].rearrange("p (k n) -> p k n", k=8)[:, :, j * 128:(j + 1) * 128],
                                  wsrc_in(l, 1032 + 512 * j + h * 128, 128)) for j in range(3)])
                Wav = Wa[:, 0:3072].rearrange("p (k n) -> p k n", k=8)
                S.op("dve", lambda e, VX=VX: e.memset(VX[:, :, 128:129], 1.0), [], [bvx])
                for i in range(NT):
                    p = i % 2
                    ts_ = slice(i * 128, (i + 1) * 128)
                    for k in range(8):
                        MM(PS[p][:, 0:384], HT[:, k, ts_], Wav[:, k, :], k == 0, k == 7, [HTb[i], bwa], [PSb[p]])
                    pqk = PS[p][:, 0:256].rearrange("p (a d) -> p a d", a=4)
                    ACT(SQ[:], PS[p][:, 0:256], AF.Square, [PSb[p]], [bsq])
                    RED(SS[:, 0:4], SQ[:].rearrange("p (a d) -> p a d", a=4), [bsq], [bss])
                    RSTD(SS[:, 0:4], SS[:, 0:4], 1.0 / 64, [bss, CBb], [bss])
                    TT(QN[:], pqk, SS[:, 0:4].unsqueeze(2).to_broadcast([128, 4, 64]), ALU.mult, [PSb[p], bss], [bqn])
                    CP(VX[:, i, 0:128], PS[p][:, 256:384], [PSb[p]], [bvx], eng="act")
                    TT(QN[:], QN[:], GQK[:, l, :, :], ALU.mult, [bqn, PRMb], [bqn])
                    TT(T1[:], QN[:, :, 0:16], ROPE[:, i, 0:16].unsqueeze(1).to_broadcast([128, 4, 16]), ALU.mult,
                       [bqn, ROPEb], [bt1])
                    TT(T2[:, :, 0:8], QN[:, :, 8:16], ROPE[:, i, 16:24].unsqueeze(1).to_broadcast([128, 4, 8]), ALU.mult,
                       [bqn, ROPEb], [bt2])
                    TT(T2[:, :, 8:16], QN[:, :, 0:8], ROPE[:, i, 24:32].unsqueeze(1).to_broadcast([128, 4, 8]), ALU.mult,
                       [bqn, ROPEb], [bt2])
                    qb3 = QB[p][:].rearrange("p (a d) -> p a d", a=4)
                    TT(qb3[:, :, 0:16], T1[:], T2[:], ALU.add, [bt1, bt2], [bqb[p]])
                    CP(qb3[:, :, 16:64], QN[:, :, 16:64], [bqn], [bqb[p]], eng="act")
                    pt = pbf(2)
                    for j in range(2):
                        TR(pt[:, j * 128:(j + 1) * 128], QB[p][:, j * 128:(j + 1) * 128], [bqb[p]], [PSb[2]])
                    CP(QKT[:, :, ts_], pt[:, 0:256].rearrange("p (j t) -> p j t", j=2), [PSb[2]], [bqkt])
                    CP(QZ[0:64, 0, ts_], pt[0:64, 0:128], [PSb[2]], [bqz])
                    CP(QZ[64:128, 1, ts_], pt[64:128, 0:128], [PSb[2]], [bqz], eng="act")
                it = 0
                for qb in range(NT):
                    p = qb % 2
                    qs_ = slice(qb * 128, (qb + 1) * 128)
                    po = PS[5 + p][:, 0:258].rearrange("p (c e) -> p c e", c=2)
                    for c in range(2):
                        csl = slice(c * 64, c * 64 + 64)
                        ng = (qb + 4) // 4
                        for g in range(ng):
                            nk = min(4, qb + 1 - 4 * g)
                            r = it % 2
                            r3 = it % 3
                            it += 1
                            for j in range(nk):
                                kb = 4 * g + j
                                MM(PS[3 + r][:, j * 128:(j + 1) * 128], QKT[:, 1, kb * 128:(kb + 1) * 128],
                                   QZ[:, c, qs_], True, True, [bqkt, bqz], [PSb[3 + r]])
                            ACT(PT[r3][:, 0:nk, :], PS[3 + r][:, 0:nk * 128].rearrange("p (j t) -> p j t", j=nk), AF.Exp,
                                [PSb[3 + r]], [bpt[r3]])
                            if g == ng - 1:
                                TT(PT[r3][:, nk - 1, :], PT[r3][:, nk - 1, :], MSKB[:], ALU.mult, [bpt[r3], CBb], [bpt[r3]])
                            for j in range(nk):
                                kb = 4 * g + j
                                MM(po[:, c, :], PT[r3][:, j, :], VX[:, kb, :], kb == 0, kb == qb, [bpt[r3], bvx], [PSb[5 + p]])
                    sml = SML[p]
                    S.op("dve", lambda e, po=po, sml=sml: e.reciprocal(out=sml[:, 0:2], in_=po[:, :, 128:129].rearrange("p c e -> p (c e)")),
                         [PSb[5 + p]], [bsmall[p]])
                    TT(sml[:, 2:3], sml[:, 1:2], NLAM[:, l, 0:1], ALU.mult, [bsmall[p], PRMb], [bsmall[p]])
                    ACT(TO[p][:], po[:, 1, 0:128], AF.Copy, [PSb[5 + p], bsmall[p]], [bto[p]], scale=sml[:, 2:3])
                    STT(OO[p][:], po[:, 0, 0:128], sml[:, 0:1], TO[p][:], ALU.mult, ALU.add, [PSb[5 + p], bsmall[p], bto[p]], [bo[p]])
                    ACT(JK[:], OO[p][:], AF.Square, [bo[p]], [bsq, bsmall[p]], accum=sml[:, 3:4])
                    RSTD(sml[:, 3:4], sml[:, 3:4], 1.0 / 128, [bsmall[p], CBb], [bsmall[p]])
                    STT(YA[p][:], OO[p][:], sml[:, 3:4], SLG[:, l, :], ALU.mult, ALU.mult, [bo[p], bsmall[p], PRMb], [bya[p]])
                    py = pbf(7)
                    TR(py[:, 0:128], YA[p][:], [bya[p]], [PSb[7]])
                    CP(MIXT[:, 2 + h, qs_], py[:, 0:128], [PSb[7]], [MXb[qb]], eng="act")

        def hgrn_phase(l):
            bs = new_phase(26)
            (bqs, bsg, bf, blf, bkk, beq, bek, besc, bsf, _u, bts, bamt, bsq, bgg, bhh, bss) = bs[0:16]
            bqh, bvh, bqkt, byb, bsb = bs[16:18], bs[18:20], bs[20:22], bs[22:24], bs[24:26]
            o = PW0
            f32t = lambda name, i: sb(name, [128, 256], F32, at=o + 1024 * i)
            QS, SGt, Ft, LF, KK, EQ, EK, SQ, GG, HHt = [f32t("H%d" % i, i) for i in range(10)]
            o2 = o + 10240
            QH = [sb("HQH%d" % i, [128, 512], BF16, at=o2 + 1024 * i) for i in range(2)]
            VH = [sb("HVH%d" % i, [128, 256], BF16, at=o2 + 2048 + 512 * i) for i in range(2)]
            QKHT = [sb("HQKT%d" % i, [64, 8, 128], BF16, at=o2 + 3072 + 2048 * i) for i in range(2)]
            ESC = sb("HESC", [64, 4, 8], F32, at=o2 + 7168)
            SF = sb("HSF", [64, 4, 64], F32, at=o2 + 7296)
            SBa = [sb("HSB%d" % i, [64, 4, 64], BF16, at=o2 + 8320 + 11936 - 8320 + 512 * i) for i in range(2)]
            TS_ = sb("HTS", [64, 4, 64], F32, at=o2 + 8832)
            AMT = sb("HAMT", [128, 4, 128], BF16, at=o2 + 9856)
            SSm = sb("HSS", [128, 8], F32, at=o2 + 10880)
            YB = [sb("HYB%d" % i, [128, 256], BF16, at=o2 + 10912 + 512 * i) for i in range(2)]
            LFH = sb("HLFH", [128, 512], BF16, at=o2 + 12960)
            blfh = Buf()
            S.alias([blfh], bs)
            pw_bufs.append(blfh)
            W1, bw1 = WLOAD([(wview(512), wsrc_in(l, 2568, 512))])
            W2, bw2 = WLOAD([(wview(512), wsrc_in(l, 3080, 512))])
            W1v, W2v = wview(512)(W1), wview(512)(W2)
            S.op("dve", lambda e: e.memset(SF[:], 0.0), [], [bsf])
            for i in range(NT):
                p = i % 2
                ts_ = slice(i * 128, (i + 1) * 128)
                for k in range(8):
                    MM(PS[0][:], HT[:, k, ts_], W1v[:, k, :], k == 0, k == 7, [HTb[i], bw1], [PSb[0]])
                for k in range(8):
                    MM(PS[1][:], HT[:, k, ts_], W2v[:, k, :], k == 0, k == 7, [HTb[i], bw2], [PSb[1]])
                ACT(QS[:], PS[0][:, 0:256], AF.Silu, [PSb[0]], [bqs])
                ACT(SGt[:], PS[0][:, 256:512], AF.Sigmoid, [PSb[0]], [bsg])
                TT(Ft[:], SGt[:], OMLB[:, l, :], ALU.mult, [bsg, PRMb], [bf])
                TT(Ft[:], Ft[:], LBB[:, l, :], ALU.add, [bf, PRMb], [bf])
                ACT(LF[:], Ft[:], AF.Ln, [bf], [blf])
                TS(KK[:], Ft[:], -1.0, ALU.mult, [bf], [bkk], s2=1.0, op1=ALU.add)
                CP(VH[p][:], PS[1][:, 0:256], [PSb[1]], [bvh[p]], eng="act")
                ACT(GG[:], PS[1][:, 256:512], AF.Silu, [PSb[1]], [bgg])
                TT(GG[:].rearrange("p (h d) -> p h d", h=4), GG[:].rearrange("p (h d) -> p h d", h=4),
                   HNG[:, l, :].unsqueeze(1).to_broadcast([128, 4, 64]), ALU.mult, [bgg, PRMb], [bgg])
                CP(LFH[:, 0:256], LF[:], [blf], [blfh])
                TT(EQ[:], LF[:], LFH[:, 0:256], ALU.subtract, [blf, blfh, beq], [beq])
                CP(LFH[:, 256:512], EQ[:], [beq], [blfh])
                for q_ in range(2):
                    MM(PS[2][:, 0:256], TSHB[:], LFH[:, 256 * q_:256 * q_ + 256], q_ == 0, q_ == 1, [CBb, blfh], [PSb[2]])
                psc = PS[3][0:64, 0:16].rearrange("p (h e) -> p h e", h=4)
                for hh in range(4):
                    for q_ in range(2):
                        MM(psc[:, hh, :], LFH[:, 256 * q_ + hh * 64:256 * q_ + (hh + 1) * 64], SELB[:], q_ == 0, q_ == 1,
                           [blfh, CBb], [PSb[3]])
                ACT(EQ[:], PS[2][:, 0:256], AF.Exp, [PSb[2]], [beq])
                ACT(EK[:], PS[2][:, 0:256], AF.Exp, [PSb[2]], [bek], scale=-1.0)
                ACT(ESC[:, :, 0:4], psc, AF.Exp, [PSb[3]], [besc])
                TT(ESC[:, :, 4:6], ESC[:, :, 0:4:2], ESC[:, :, 1:4:2], ALU.mult, [besc], [besc])
                TT(QH[p][:, 0:256], QS[:], EQ[:], ALU.mult, [bqs, beq], [bqh[p]])
                TT(QH[p][:, 256:512], KK[:], EK[:], ALU.mult, [bkk, bek], [bqh[p]])
                pt = pbf(4)[0:64, :]
                for j in range(8):
                    TR(pt[:, j * 128:(j + 1) * 128], QH[p][:, j * 64:(j + 1) * 64], [bqh[p]], [PSb[4]])
                CP(QKHT[p][:], pt.rearrange("p (j t) -> p j t", j=8), [PSb[4]], [bqkt[p]])
                for hh in range(4):
                    MM(PS[5][:, hh * 128:(hh + 1) * 128], QKHT[p][:, 4 + hh, :], QKHT[p][:, hh, :], True, True,
                       [bqkt[p]], [PSb[5]])
                TT(AMT[:], PS[5][:].rearrange("p (h t) -> p h t", h=4), MSKHB[:].unsqueeze(1).to_broadcast([128, 4, 128]),
                   ALU.mult, [PSb[5], CBb], [bamt])
                po = PS[6][:, 0:256].rearrange("p (h d) -> p h d", h=4)
                for a in range(2):
                    asl = slice(a * 64, a * 64 + 64)
                    TT(SBa[a][:], SF[:], ESC[:, :, 2 * a:2 * a + 1].to_broadcast([64, 4, 64]), ALU.mult, [bsf, besc], [bsb[a]])
                    ps2 = PS[7][0:64, 0:256].rearrange("p (h d) -> p h d", h=4)
                    for hh in range(4):
                        MM(ps2[:, hh, :], QH[p][asl, 256 + hh * 64:256 + (hh + 1) * 64], VH[p][asl, hh * 64:(hh + 1) * 64],
                           True, True, [bqh[p], bvh[p]], [PSb[7]])
                    TT(TS_[:], ps2, ESC[:, :, 2 * a + 1:2 * a + 2].to_broadcast([64, 4, 64]), ALU.mult, [PSb[7], besc], [bts])
                    TT(SF[:], SF[:], ESC[:, :, 4 + a:5 + a].to_broadcast([64, 4, 64]), ALU.mult, [bsf, besc], [bsf])
                    TT(SF[:], SF[:], TS_[:], ALU.add, [bsf, bts], [bsf])
                for hh in range(4):
                    MM(po[:, hh, :], AMT[:, hh, :], VH[p][:, hh * 64:(hh + 1) * 64], True, False, [bamt, bvh[p]], [PSb[6]])
                    for a in range(2):
                        asl = slice(a * 64, a * 64 + 64)
                        MM(po[asl, hh, :], QKHT[p][:, hh, asl], SBa[a][:, hh, :], False, True, [bqkt[p], bsb[a]], [PSb[6]])
                ACT(SQ[:], PS[6][:, 0:256], AF.Square, [PSb[6]], [bsq])
                RED(SSm[:, 0:4], SQ[:].rearrange("p (h d) -> p h d", h=4), [bsq], [bss])
                RSTD(SSm[:, 0:4], SSm[:, 0:4], 1.0 / 64, [bss, CBb], [bss])
                TT(HHt[:].rearrange("p (h d) -> p h d", h=4), po, SSm[:, 0:4].unsqueeze(2).to_broadcast([128, 4, 64]),
                   ALU.mult, [PSb[6], bss], [bhh])
                TT(YB[p][:], HHt[:], GG[:], ALU.mult, [bhh, bgg], [byb[p]])
                py = pbf(3)
                for j in range(2):
                    TR(py[:, 512 + j * 128:512 + (j + 1) * 128], YB[p][:, j * 128:(j + 1) * 128], [byb[p]], [PSb[3]])
                CP(MIXT[:, 6:8, ts_], py[:, 512:768].rearrange("p (j t) -> p j t", j=2), [PSb[3]], [MXb[i]], eng="act")

        def outproj_phase(l):
            ws = []
            for nh in range(2):
                ws.append(WLOAD([(wview(512), w_out[l, :, nh * 512:(nh + 1) * 512].rearrange("(k p) n -> p k n", p=128))]))
            it = 0
            for i in range(NT):
                for nh in range(2):
                    b_ = it % 4
                    it += 1
                    wv_ = wview(512)(ws[nh][0])
                    for k in range(8):
                        MM(PS[b_][:], MIXT[:, k, i * 128:(i + 1) * 128], wv_[:, k, :], k == 0, k == 7,
                           [MXb[i], ws[nh][1]], [PSb[b_]])
                    TT(X[:, i, nh * 512:(nh + 1) * 512], PS[b_][:], X[:, i, nh * 512:(nh + 1) * 512], ALU.add,
                       [PSb[b_], Xb[i]], [Xb[i]])

        def ffn_phase(l):
            bs = new_phase(2)
            br = bs
            R = [sb("FR%d" % i, [128, 512], F32, at=PW0 + 8192 + 2048 * i) for i in range(2)]
            if T >= 2048:
                AT = MIXT[:].rearrange("p k t -> p (k t)")
                ATv = AT[:, 0:32 * 512].rearrange("p (j t) -> p j t", j=32)
            else:
                ATv = sb("ATs", [128, 32, 512], BF16, at=PW0 + 36000)[:]
            bat = [Buf() for _ in range(32)]
            it = 0
            for G in range(NG):
                S.alias(bat, MXb + bat)
                for jg in range(8):
                    wu, bwu = WLOAD([(wview(512), w_up[l, :, jg * 512:(jg + 1) * 512].rearrange("(k p) n -> p k n", p=128))])
                    wuv = wview(512)(wu)
                    for jj in range(4):
                        j = jg * 4 + jj
                        p = it % 2
                        it += 1
                        for k in range(8):
                            MM(PS[p][:], wuv[:, k, jj * 128:(jj + 1) * 128], HT[:, k, G * 512:(G + 1) * 512], k == 0, k == 7,
                               [bwu] + HTb[4 * G:4 * G + 4], [PSb[p]])
                        ACT(R[p][:], PS[p][:], AF.Relu, [PSb[p]], [br[p]])
                        TT(ATv[:, j, :], R[p][:], R[p][:], ALU.mult, [br[p]], [bat[j]])
                for nh in range(2):
                    for jg in range(4):
                        wd, bwd = WLOAD([(wview(512), w_dn[l, jg * 1024:(jg + 1) * 1024, nh * 512:(nh + 1) * 512]
                                          .rearrange("(j p) n -> p j n", p=128))])
                        wdv = wview(512)(wd)
                        for ti in range(4):
                            for jj in range(8):
                                j = jg * 8 + jj
                                MM(PS[2 + ti][:], ATv[:, j, ti * 128:(ti + 1) * 128], wdv[:, jj, :], j == 0, j == 31,
                                   [bat[j], bwd], [PSb[2 + ti]])
                    for ti in range(4):
                        i = G * 4 + ti
                        TT(X[:, i, nh * 512:(nh + 1) * 512], PS[2 + ti][:], X[:, i, nh * 512:(nh + 1) * 512], ALU.add,
                           [PSb[2 + ti], Xb[i]], [Xb[i]])
            S.alias(MXb, bat)

        for s in range(NSEQ):
            for i in range(NT):
                S.dma("sp", lambda e, s=s, i=i: e.dma_start(out=X[:, i, :], in_=x_d[s * T + i * 128:s * T + (i + 1) * 128, :]),
                      "ldx%d" % i, writes=[Xb[i]])
            for l in range(DEPTH):
                if stop >= 1:
                    norm_phase(G1C, l)
                if stop >= 2:
                    mlstm_phase(l)
                if stop >= 3:
                    attn_phase(l)
                if stop >= 4:
                    hgrn_phase(l)
                if stop < 5:
                    continue
                if dbg and s == 0 and l == 0:
                    S.dma("sp", lambda e: e.dma_start(out=dbg_d[:, :, :], in_=MIXT[:]), "std", reads=MXb)
                outproj_phase(l)
                norm_phase(G2C, l)
                ffn_phase(l)
            for i in range(NT):
                S.dma("sp", lambda e, s=s, i=i: e.dma_start(out=y_d[s * T + i * 128:s * T + (i + 1) * 128, :], in_=X[:, i, :]),
                      "sty%d" % i, reads=[Xb[i]])
        S.wait_all("sp", Xb + MXb)
        global _LAST_S
        _LAST_S = S
        S.emit()
    return nc


_CACHE = {}


def kernel(**inputs):
    x = np.ascontiguousarray(inputs["x"], dtype=np.float32)
    B, T, _ = x.shape
    NCORE = 8
    NSEQ = B // NCORE
    DEPTH = inputs["w_in"].shape[0]
    key = (T, NSEQ, DEPTH)
    if key not in _CACHE:
        _CACHE[key] = build(T, NSEQ, DEPTH)
    nc = _CACHE[key]
    cst, rope = host_consts(T)
    shared = {k: np.ascontiguousarray(v, dtype=np.float32) for k, v in inputs.items() if k != "x"}
    shared["cst"] = cst
    shared["rope"] = rope
    in_maps = []
    for c in range(NCORE):
        m = dict(shared)
        m["x"] = x[c * NSEQ:(c + 1) * NSEQ].reshape(NSEQ * T, D)
        in_maps.append(m)
    res = run_bass_kernel_spmd(nc, in_maps, core_ids=list(range(NCORE)))
    out = np.concatenate([r["y"].reshape(NSEQ, T, D) for r in res.results], axis=0)
    return out.astype(np.float32)
```

```python
import math
import numpy as np
from contextlib import ExitStack
import concourse.bass as bass
import concourse.mybir as mybir
from concourse.bass_utils import run_bass_kernel_spmd

F32 = mybir.dt.float32
BF16 = mybir.dt.bfloat16
AF = mybir.ActivationFunctionType
ALU = mybir.AluOpType
AX = mybir.AxisListType
D = 1024
DIN = 3592
DFF = 4096
EPS = 1e-6
ENGS = ("pe", "act", "dve", "pool", "sp")
import os
FLAT = bool(int(os.environ.get('PIPE_FLAT', '0')))


class Buf:
    __slots__ = ("lw", "rd")

    def __init__(self):
        self.lw = None
        self.rd = {}


class Sched:
    def __init__(self, nc, stack):
        self.nc, self.stack = nc, stack
        self.prog = {e: [] for e in ENGS}
        self.sems, self.cnt, self.waited = {}, {}, {}
        for e in ENGS:
            self.new_sem(e)

    def new_sem(self, key):
        self.sems[key] = self.stack.enter_context(self.nc.semaphore("s_" + str(key)))
        self.cnt[key] = 0
        return key

    def _deps(self, reads, writes):
        deps = {}
        for b in reads:
            if b.lw is not None and deps.get(b.lw[0], 0) < b.lw[1]:
                deps[b.lw[0]] = b.lw[1]
        for b in writes:
            if b.lw is not None and deps.get(b.lw[0], 0) < b.lw[1]:
                deps[b.lw[0]] = b.lw[1]
            for k, v in b.rd.items():
                if deps.get(k, 0) < v:
                    deps[k] = v
        return deps

    def _waits(self, eng, deps):
        ws = []
        for k, v in deps.items():
            if self.waited.get((eng, k), 0) < v:
                self.waited[(eng, k)] = v
                ws.append((k, v))
                if k in self.prog:
                    self.prog[k][v - 1][3] = True
        return ws

    def op(self, eng, fn, reads=(), writes=()):
        deps = self._deps(reads, writes)
        if eng == "pe":
            deps.pop("pe", None)
        ws = self._waits(eng, deps)
        self.prog[eng].append([ws, fn, eng, False])
        n = len(self.prog[eng])
        for b in reads:
            b.rd[eng] = n
        for b in writes:
            b.lw = (eng, n)
            b.rd = {}

    def dma(self, queue, fns, semkey, reads=(), writes=()):
        if not isinstance(fns, (list, tuple)):
            fns = [fns]
        deps = self._deps(reads, writes)
        ws = self._waits(queue, deps)
        for j, fn in enumerate(fns):
            self.cnt[semkey] += 16
            self.prog[queue].append([ws if j == 0 else [], fn, semkey, True])
        n = self.cnt[semkey]
        for b in reads:
            b.rd[semkey] = n
        for b in writes:
            b.lw = (semkey, n)
            b.rd = {}

    def alias(self, new_bufs, old_bufs):
        m = {}
        for b in old_bufs:
            if b.lw is not None and m.get(b.lw[0], 0) < b.lw[1]:
                m[b.lw[0]] = b.lw[1]
            for k, v in b.rd.items():
                if m.get(k, 0) < v:
                    m[k] = v
        for b in new_bufs:
            b.lw = None
            b.rd = dict(m)

    def wait_all(self, eng, bufs):
        deps = self._deps(bufs, bufs)
        ws = self._waits(eng, deps)
        self.prog[eng].append([ws, None, None, False])

    def emit(self):
        nc = self.nc
        names = {"pe": "tensor", "act": "scalar", "dve": "vector", "pool": "gpsimd", "sp": "sync"}
        pref = {}
        for e in ENGS:
            c, arr = 0, []
            for ent in self.prog[e]:
                if ent[3] and ent[2] == e:
                    c += 1
                arr.append(c)
            pref[e] = arr
        sems = self.sems
        with nc.Block() as block:
            for e in ENGS:
                def body(eng, e=e):
                    for ws, fn, inckey, flagged in self.prog[e]:
                        for k, v in ws:
                            eng.wait_ge(sems[k], pref[k][v - 1] if k in pref else v)
                        if fn is not None:
                            ins = fn(eng)
                            if inckey == e:
                                if flagged:
                                    ins.then_inc(sems[e], 1)
                            else:
                                ins.then_inc(sems[inckey], 16)
                getattr(block, names[e])(body)


def host_consts(T):
    NT = T // 128
    c = np.zeros((128, 648), np.float32)
    i = np.arange(128)
    c[:, 0:128] = np.eye(128)
    c[:, 128:256] = (i[:, None] <= i[None, :])
    c[:, 256:384] = 1.0
    sl, tl = i[:, None] % 64, i[None, :] % 64
    same = (i[:, None] // 64) == (i[None, :] // 64)
    c[:, 384:512] = same * ((sl <= tl).astype(np.float32) - (sl <= 31))
    c[:, 512:640] = same * (sl <= tl)
    for a in range(2):
        c[:, 640 + 2 * a] = ((i // 64) == a) & ((i % 64) <= 31)
        c[:, 641 + 2 * a] = ((i // 64) == a) & ((i % 64) > 31)
    half = 8
    inv = (500000.0 ** (-np.arange(half, dtype=np.float32) / half)).astype(np.float32)
    ang = np.arange(T, dtype=np.float32)[:, None] * inv[None, :]
    cos, sin = np.cos(ang).astype(np.float32), np.sin(ang).astype(np.float32)
    tab = np.concatenate([cos, cos, -sin, sin], axis=1)
    rope = np.ascontiguousarray(tab.reshape(NT, 128, 32).transpose(1, 0, 2))
    return c, rope


def build(T=2048, NSEQ=2, DEPTH=2, dbg=False, stop=99, sub=0):
    NT = T // 128
    NG = T // 512
    nc = bass.Bass("TRN2", target_bir_lowering=False)
    dt_in = lambda name, shape: nc.dram_tensor(name, shape, F32, kind="ExternalInput").ap()
    x_d = dt_in("x", [NSEQ * T, D])
    w_in = dt_in("w_in", [DEPTH, D + 1, DIN])
    w_out = dt_in("w_out", [DEPTH, D + 1, D])
    w_up = dt_in("w_up", [DEPTH, D + 1, DFF])
    w_dn = dt_in("w_down", [DEPTH, DFF + 1, D])
    n1g = dt_in("norm1_g", [DEPTH, D])
    n2g = dt_in("norm2_g", [DEPTH, D])
    convw = dt_in("mlstm_conv_w", [DEPTH, 4, 512])
    gateb = dt_in("mlstm_gate_b", [DEPTH, 2, 4])
    mng = dt_in("mlstm_norm_g", [DEPTH, 64])
    qng = dt_in("diff_q_norm_g", [DEPTH, 64])
    kng = dt_in("diff_k_norm_g", [DEPTH, 64])
    dlam = dt_in("diff_lambda", [DEPTH, 4, 64])
    slg = dt_in("diff_subln_g", [DEPTH, 128])
    lbp = dt_in("hgrn_lb_param", [DEPTH, 256])
    hng = dt_in("hgrn_norm_g", [DEPTH, 64])
    cst_d = dt_in("cst", [128, 648])
    rope_d = dt_in("rope", [128, NT, 32])
    y_d = nc.dram_tensor("y", [NSEQ * T, D], F32, kind="ExternalOutput").ap()
    if dbg:
        dbg_d = nc.dram_tensor("dbg", [128, 8, T], BF16, kind="ExternalOutput").ap()

    with ExitStack() as st:
        S = Sched(nc, st)
        off = [16512]

        def sb(name, shape, dt, at=None):
            n = 1
            for s_ in shape[1:]:
                n *= s_
            nbytes = n * (2 if dt == BF16 else 4)
            nbytes = (nbytes + 31) // 32 * 32
            if at is None:
                at = off[0]
                off[0] += nbytes
            return nc.alloc_sbuf_tensor_at(name, list(shape), dt, offset=at)

        X = sb("X", [128, NT, D], F32)
        Xb = [Buf() for _ in range(NT)]
        HT = sb("HT", [128, 8, T], BF16)
        HTb = [Buf() for _ in range(NT)]
        MIXT = sb("MIXT", [128, 8, T], BF16)
        MXb = [Buf() for _ in range(NT)]
        NS = 3
        RING = [sb("ring%d" % i, [128, 4096], BF16) for i in range(NS)]
        RGb = [Buf() for _ in range(NS)]
        for i in range(NS):
            S.new_sem("w%d" % i)
        CST = sb("CST", [128, 648], F32); CSTb = Buf()
        ROPE = sb("ROPE", [128, NT, 32], F32); ROPEb = Buf()
        IDB = sb("IDB", [128, 128], BF16)
        MSKB = sb("MSKB", [128, 128], BF16)
        MSKHB = sb("MSKHB", [128, 128], BF16)
        ONESB = sb("ONESB", [128, 128], BF16)
        TSHB = sb("TSHB", [128, 128], BF16)
        SELB = sb("SELB", [128, 4], BF16)
        CBb = Buf()
        EPSt = sb("EPSt", [128, 1], F32)
        ONEt = sb("ONEt", [128, 1], F32)
        G1C = sb("G1C", [128, DEPTH, 8], F32)
        G2C = sb("G2C", [128, DEPTH, 8], F32)
        CW = sb("CW", [128, DEPTH, 4, 4], F32)
        GBB = sb("GBB", [128, DEPTH, 8], F32)
        MNG = sb("MNG", [128, DEPTH, 64], F32)
        GQK = sb("GQK", [128, DEPTH, 4, 64], F32)
        SLG = sb("SLG", [128, DEPTH, 128], F32)
        HNG = sb("HNG", [128, DEPTH, 64], F32)
        LBP = sb("LBP", [128, DEPTH, 256], F32)
        LBB = sb("LBB", [128, DEPTH, 256], F32)
        OMLB = sb("OMLB", [128, DEPTH, 256], F32)
        LAMI = sb("LAMI", [128, DEPTH, 4, 64], F32)
        NLAM = sb("NLAM", [128, DEPTH, 2], F32)
        WG = sb("WG", [128, DEPTH, 8, 8], BF16)
        PRMb = Buf()
        S.new_sem("ldc"); S.new_sem("ldr"); S.new_sem("ldp"); S.new_sem("ldg"); S.new_sem("std")
        for i in range(NT):
            S.new_sem("ldx%d" % i); S.new_sem("sty%d" % i)
        WGb = Buf()
        pfns, gfns = [], []
        PW0 = off[0]
        PWEND = 229344
        assert PW0 + 36000 <= PWEND, (PW0, PWEND)

        PS = [nc.alloc_psum_tensor("ps%d" % i, [128, 512], F32) for i in range(8)]
        PSb = [Buf() for _ in range(8)]

        def pbf(i):
            return PS[i][:].bitcast(BF16)

        def ACT(out, in_, func, reads, writes, bias=None, scale=None, accum=None):
            kw = {}
            if bias is not None:
                kw["bias"] = bias
            if scale is not None:
                kw["scale"] = scale
            if accum is not None:
                kw["accum_out"] = accum
            S.op("act", lambda e: e.activation(out=out, in_=in_, func=func, **kw), reads, writes)

        def TT(out, in0, in1, op, reads, writes, eng="dve"):
            S.op(eng, lambda e: e.tensor_tensor(out=out, in0=in0, in1=in1, op=op), reads, writes)

        def TS(out, in0, s1, op0, reads, writes, s2=None, op1=None, eng="dve"):
            if op1 is None:
                S.op(eng, lambda e: e.tensor_scalar(out=out, in0=in0, scalar1=s1, scalar2=None, op0=op0), reads, writes)
            else:
                S.op(eng, lambda e: e.tensor_scalar(out=out, in0=in0, scalar1=s1, scalar2=s2, op0=op0, op1=op1), reads, writes)

        def STT(out, in0, scalar, in1, op0, op1, reads, writes):
            S.op("dve", lambda e: e.scalar_tensor_tensor(out=out, in0=in0, scalar=scalar, in1=in1, op0=op0, op1=op1),
                 reads, writes)

        def CP(out, in_, reads, writes, eng="dve"):
            if eng == "act":
                S.op("act", lambda e: e.copy(out=out, in_=in_), reads, writes)
            else:
                S.op(eng, lambda e: e.tensor_copy(out=out, in_=in_), reads, writes)

        def RECIP(out, in_, reads, writes):
            S.op("dve", lambda e: e.reciprocal(out=out, in_=in_), reads, writes)

        def RED(out, in_, reads, writes):
            S.op("dve", lambda e: e.tensor_reduce(out=out, in_=in_, axis=AX.X, op=ALU.add), reads, writes)

        def MM(out, lhsT, rhs, start, stop, reads, writes):
            S.op("pe", lambda e: e.matmul(out, lhsT=lhsT, rhs=rhs, start=start, stop=stop), reads, writes)

        def TR(out, in_, reads, writes):
            S.op("pe", lambda e: e.transpose(out=out, in_=in_, identity=IDB[:]), list(reads) + [CBb], writes)

        def RSTD(out, in_, scale, reads, writes):
            P = out.shape[0]
            ACT(out, in_, AF.Sqrt, reads, writes, bias=EPSt[0:P, :], scale=scale)
            RECIP(out, out, writes, writes)

        wctr = [0]

        def WLOAD(parts):
            s = wctr[0] % NS
            wctr[0] += 1
            fns = []
            for dst, src in parts:
                for k in range(8):
                    fns.append(lambda e, dst=dst, src=src, k=k: e.dma_start(out=dst(RING[s])[:, k, :], in_=src[:, k, :]))
            S.dma("pool", fns, "w%d" % s, writes=[RGb[s]])
            return RING[s], RGb[s]

        S.dma("sp", lambda e: e.dma_start(out=CST[:], in_=cst_d[:, :]), "ldc", writes=[CSTb])
        S.dma("sp", lambda e: e.dma_start(out=ROPE[:], in_=rope_d[:, :, :]), "ldr", writes=[ROPEb])
        CP(IDB[:], CST[:, 0:128], [CSTb], [CBb])
        CP(MSKB[:], CST[:, 128:256], [CSTb], [CBb])
        CP(MSKHB[:], CST[:, 512:640], [CSTb], [CBb])
        CP(ONESB[:], CST[:, 256:384], [CSTb], [CBb])
        CP(TSHB[:], CST[:, 384:512], [CSTb], [CBb])
        CP(SELB[:], CST[:, 640:644], [CSTb], [CBb])
        S.op("dve", lambda e: e.memset(EPSt[:], EPS), [], [CBb])
        S.op("dve", lambda e: e.memset(ONEt[:], 1.0), [], [CBb])
        for l in range(DEPTH):
            for k in range(8):
                for (dst, src) in ((G1C, n1g), (G2C, n2g)):
                    pfns.append(lambda e, dst=dst, src=src, l=l, k=k: e.dma_start(
                        out=dst[:, l, k:k + 1], in_=src[l:l + 1, k * 128:(k + 1) * 128].rearrange("o p -> p o")))
            for ct in range(4):
                for j in range(4):
                    pfns.append(lambda e, l=l, ct=ct, j=j: e.dma_start(
                        out=CW[:, l, ct, j:j + 1],
                        in_=convw[l, j:j + 1, ct * 128:(ct + 1) * 128].rearrange("o p -> p o")))
            bc = lambda ap, n: ap.to_broadcast([128, n])
            pfns.append(lambda e, l=l: e.dma_start(out=GBB[:, l, :], in_=bc(gateb[l:l + 1].rearrange("o a b -> o (a b)"), 8)))
            pfns.append(lambda e, l=l: e.dma_start(out=MNG[:, l, :], in_=bc(mng[l:l + 1, :], 64)))
            for j in range(2):
                pfns.append(lambda e, l=l, j=j: e.dma_start(out=GQK[:, l, j, :], in_=bc(qng[l:l + 1, :], 64)))
                pfns.append(lambda e, l=l, j=j: e.dma_start(out=GQK[:, l, 2 + j, :], in_=bc(kng[l:l + 1, :], 64)))
            pfns.append(lambda e, l=l: e.dma_start(out=SLG[:, l, :], in_=bc(slg[l:l + 1, :], 128)))
            pfns.append(lambda e, l=l: e.dma_start(out=HNG[:, l, :], in_=bc(hng[l:l + 1, :], 64)))
            pfns.append(lambda e, l=l: e.dma_start(out=LAMI[:, l, :, :].rearrange("p a b -> p (a b)"),
                                                   in_=bc(dlam[l:l + 1].rearrange("o a b -> o (a b)"), 256)))
            pfns.append(lambda e, l=l: e.dma_start(out=LBP[:, l, :], in_=bc(lbp[l:l + 1, :], 256)))
            for k in range(8):
                gfns.append(lambda e, l=l, k=k: e.dma_start(
                    out=WG[:, l, k, :], in_=w_in[l, k * 128:(k + 1) * 128, 1024:1032]))
        S.dma("sp", pfns, "ldp", writes=[PRMb])
        S.dma("pool", gfns, "ldg", writes=[WGb])
        lam_init = [0.8 - 0.6 * math.exp(-0.3 * l) for l in range(DEPTH)]
        PT0 = sb("PT0", [128, DEPTH, 256], F32, at=PW0)
        PT1 = sb("PT1", [128, 256], F32, at=PW0 + DEPTH * 1024)
        PT2 = sb("PT2", [128, 8], F32, at=PW0 + DEPTH * 1024 + 1024)
        PTb = Buf()
        pw_bufs = [PTb]
        ACT(PT0[:], LBP[:], AF.Exp, [PRMb], [PTb])
        CP(PT1[:], PT0[:, 0, :], [PTb], [PTb])
        for l in range(1, DEPTH):
            TT(PT1[:], PT1[:], PT0[:, l, :], ALU.add, [PTb], [PTb])
        RECIP(PT1[:], PT1[:], [PTb], [PTb])
        for l in range(DEPTH):
            TT(PT0[:, l, :], PT0[:, l, :], PT1[:], ALU.mult, [PTb], [PTb])
        CP(LBB[:, 0, :], PT0[:, 0, :], [PTb], [PRMb])
        for l in range(1, DEPTH):
            TT(LBB[:, l, :], LBB[:, l - 1, :], PT0[:, l, :], ALU.add, [PTb, PRMb], [PRMb])
        for l in range(DEPTH):
            TT(LBB[:, l, :], LBB[:, l, :], PT0[:, 0, :], ALU.subtract, [PTb, PRMb], [PRMb])
            TS(OMLB[:, l, :], LBB[:, l, :], -1.0, ALU.mult, [PRMb], [PRMb], s2=1.0, op1=ALU.add)
        for l in range(DEPTH):
            TT(PT1[:, 0:64], LAMI[:, l, 0, :], LAMI[:, l, 1, :], ALU.mult, [PRMb, PTb], [PTb])
            TT(PT1[:, 64:128], LAMI[:, l, 2, :], LAMI[:, l, 3, :], ALU.mult, [PRMb, PTb], [PTb])
            RED(PT2[:, 0:2], PT1[:, 0:128].rearrange("p (a b) -> p a b", a=2), [PTb], [PTb])
            ACT(PT2[:, 0:2], PT2[:, 0:2], AF.Exp, [PTb], [PTb])
            TT(PT2[:, 2:3], PT2[:, 1:2], PT2[:, 0:1], ALU.subtract, [PTb], [PTb])
            TS(NLAM[:, l, 0:1], PT2[:, 2:3], -lam_init[l], ALU.add, [PTb, PRMb], [PRMb])
            TS(GQK[:, l, 0:2, :], GQK[:, l, 0:2, :], 0.125, ALU.mult, [PRMb], [PRMb])
            TS(SLG[:, l, :], SLG[:, l, :], 1.0 - lam_init[l], ALU.mult, [PRMb], [PRMb])

        def new_phase(n):
            bs = [Buf() for _ in range(n)]
            S.alias(bs, pw_bufs)
            pw_bufs[:] = bs
            return bs

        def norm_phase(GC, l):
            bs = new_phase(5)
            bj, bss, bxs0, bxs1, brs = bs
            JUNK = sb("JUNK", [128, D], BF16, at=PW0)
            XS = [sb("XS%d" % i, [128, D], BF16, at=PW0 + 2048 + 2048 * i) for i in range(2)]
            SSQ = sb("SSQ", [128, NT], F32, at=PW0 + 6144)
            RS = sb("RS", [128, NT], F32, at=PW0 + 6144 + 128)
            bxs = [bxs0, bxs1]
            for i in range(NT):
                ACT(JUNK[:], X[:, i, :], AF.Square, [Xb[i]], [bj, bss], accum=SSQ[:, i:i + 1])
            ACT(RS[:], SSQ[:], AF.Sqrt, [bss, CBb], [brs], bias=EPSt[:], scale=1.0 / D)
            RECIP(RS[:], RS[:], [brs], [brs])
            for i in range(NT):
                p = i % 2
                ACT(XS[p][:], X[:, i, :], AF.Copy, [Xb[i], brs], [bxs[p]], scale=RS[:, i:i + 1])
                pt = pbf(p)
                for k in range(8):
                    TR(pt[:, k * 128:(k + 1) * 128], XS[p][:, k * 128:(k + 1) * 128], [bxs[p]], [PSb[p]])
                TT(HT[:, :, i * 128:(i + 1) * 128], pt.rearrange("p (k t) -> p k t", k=8),
                   GC[:, l, :].unsqueeze(2).to_broadcast([128, 8, 128]), ALU.mult, [PSb[p], PRMb], [HTb[i]])

        def wsrc_in(l, c0, n):
            return w_in[l, 0:D, c0:c0 + n].rearrange("(k p) n -> p k n", p=128)

        def wview(ncols):
            return lambda t: t[:, 0:8 * ncols].rearrange("p (k n) -> p k n", k=8)

        def mlstm_phase(l):
            bs = new_phase(12)
            bqk = bs[0:4]
            bpre, bacc, bsg, bhalo = bs[4:6], bs[6:8], bs[8], bs[9]
            QKT = sb("QKT", [128, 4, T], BF16, at=PW0)
            o = PW0 + 8 * T
            PRE = [sb("PRE%d" % i, [128, 515], F32, at=o + 2080 * i) for i in range(2)]
            ACC = [sb("ACC%d" % i, [128, 512], F32, at=o + 4160 + 2048 * i) for i in range(2)]
            SG = sb("SGm", [128, 512], F32, at=o + 8256)
            HALO = sb("HALO", [128, 4, 3], F32, at=o + 10304)
            Wqk, bw = WLOAD([(wview(512), wsrc_in(l, 0, 512))])
            Wqv = wview(512)(Wqk)
            S.op("dve", lambda e: e.memset(HALO[:], 0.0), [], [bhalo])
            it = 0
            for G in range(NG):
                for ct in range(4):
                    p = it % 2
                    it += 1
                    pb_ = 6 + p
                    for k in range(8):
                        MM(PS[pb_][:], Wqv[:, k, ct * 128:(ct + 1) * 128], HT[:, k, G * 512:(G + 1) * 512],
                           k == 0, k == 7, [bw] + HTb[4 * G:4 * G + 4], [PSb[pb_]])
                    CP(PRE[p][:, 0:3], HALO[:, ct, :], [bhalo], [bpre[p]])
                    CP(PRE[p][:, 3:515], PS[pb_][:], [PSb[pb_]], [bpre[p]], eng="act")
                    CP(HALO[:, ct, :], PRE[p][:, 512:515], [bpre[p]], [bhalo])
                    TS(ACC[p][:], PRE[p][:, 0:512], CW[:, l, ct, 0:1], ALU.mult, [bpre[p], PRMb], [bacc[p]])
                    for j in range(1, 4):
                        STT(ACC[p][:], PRE[p][:, j:j + 512], CW[:, l, ct, j:j + 1], ACC[p][:], ALU.mult, ALU.add,
                            [bpre[p], PRMb, bacc[p]], [bacc[p]])
                    if ct < 2:
                        ACT(QKT[:, ct, G * 512:(G + 1) * 512], ACC[p][:], AF.Silu, [bacc[p]], [bqk[ct]])
                    else:
                        ACT(SG[:], ACC[p][:], AF.Sigmoid, [bacc[p]], [bsg])
                        STT(QKT[:, ct, G * 512:(G + 1) * 512], ACC[p][:], 0.125, SG[:], ALU.mult, ALU.mult,
                            [bacc[p], bsg], [bqk[ct]])
            if sub == 1:
                return
            old = [bpre[0], bpre[1], bacc[0], bacc[1], bsg, bhalo]
            nb = [Buf() for _ in range(16)]
            S.alias(nb, old)
            pw_bufs.extend(nb)
            bsm, bvbe, bkt, bsmt = nb[0:2], nb[2:4], nb[4:6], nb[6:8]
            bct, bctb, btc, bhh, bog, byb = nb[8], nb[9], nb[10], nb[11], nb[12], nb[13:15]
            SM = [sb("SMm%d" % i, [128, 64], F32, at=o + 256 * i) for i in range(2)]
            VBE = [sb("VBE%d" % i, [128, 4, 65], BF16, at=o + 512 + 544 * i) for i in range(2)]
            KTK = [sb("KTK%d" % i, [128, 256], BF16, at=o + 1600 + 512 * i) for i in range(2)]
            SMT = [sb("SMT%d" % i, [128, 4, 128], BF16, at=o + 2624 + 1024 * i) for i in range(2)]
            CTF = sb("CTF", [128, 2, 65], F32, at=o + 4672)
            CTB = sb("CTB", [128, 2, 65], BF16, at=o + 5216)
            TC = sb("TCm", [128, 2, 65], F32, at=o + 5504)
            HH = sb("HHm", [128, 256], F32, at=o + 6048)
            SQ = sb("SQm", [128, 256], F32, at=o + 7072)
            OG = sb("OGm", [128, 256], F32, at=o + 8096)
            YB = [sb("YBm%d" % i, [128, 256], BF16, at=o + 9120 + 512 * i) for i in range(2)]
            HLm = [sb("HLm%d" % i, [128, 8], BF16, at=o + 10144 + 32 * i) for i in range(2)]
            QZ = [sb("QZm%d" % i, [128, 4, 128], BF16, at=o + 10240 + 1024 * i) for i in range(2)]
            bqz = [Buf(), Buf()]
            S.alias(bqz, old)
            pw_bufs.extend(bqz)
            for i_ in range(2):
                S.op("dve", lambda e, i_=i_: e.memset(QZ[i_][:], 0.0), [], [bqz[i_]])
            bhl = [Buf(), Buf()]
            S.alias(bhl, old)
            pw_bufs.extend(bhl)
            Wvo, bwv = WLOAD([(wview(512), wsrc_in(l, 512, 512))])
            Wvv = wview(512)(Wvo)
            S.op("dve", lambda e: e.memset(CTF[:], 0.0), [], [bct])
            S.op("dve", lambda e: e.memset(CTB[:], 0.0), [], [bctb])
            for c in range(NT):
                p = c % 2
                cs = slice(c * 128, (c + 1) * 128)
                pvo = 0 if p == 0 else 5
                sm = SM[p]
                for k in range(8):
                    MM(PS[pvo][:], HT[:, k, cs], Wvv[:, k, :], k == 0, k == 7, [HTb[c], bwv], [PSb[pvo]])
                for k in range(8):
                    MM(PS[1][:, 0:8], HT[:, k, cs], WG[:, l, k, :], k == 0, k == 7, [HTb[c], WGb], [PSb[1]])
                TT(sm[:, 0:8], PS[1][:, 0:8], GBB[:, l, :], ALU.add, [PSb[1], PRMb], [bsm[p]])
                ACT(sm[:, 8:12], sm[:, 4:8], AF.Exp, [bsm[p]], [bsm[p]], scale=-1.0)
                ACT(sm[:, 8:12], sm[:, 8:12], AF.Ln, [bsm[p], CBb], [bsm[p]], bias=ONEt[:])
                hl_ = HLm[p]
                CP(hl_[:, 0:4], sm[:, 8:12], [bsm[p]], [bhl[p]])
                TT(sm[:, 48:52], sm[:, 8:12], hl_[:, 0:4], ALU.subtract, [bsm[p], bhl[p]], [bsm[p]])
                CP(hl_[:, 4:8], sm[:, 48:52], [bsm[p]], [bhl[p]])
                for q_ in range(2):
                    MM(PS[1][:, 8:12], MSKB[:], hl_[:, 4 * q_:4 * q_ + 4], q_ == 0, q_ == 1, [CBb, bhl[p]], [PSb[1]])
                for q_ in range(2):
                    MM(PS[1][:, 12:16], ONESB[:], hl_[:, 4 * q_:4 * q_ + 4], q_ == 0, q_ == 1, [CBb, bhl[p]], [PSb[1]])
                TT(sm[:, 12:16], sm[:, 0:4], PS[1][:, 8:12], ALU.add, [bsm[p], PSb[1]], [bsm[p]])
                ACT(sm[:, 16:20], sm[:, 12:16], AF.Exp, [bsm[p]], [bsm[p]])
                ACT(sm[:, 20:28], PS[1][:, 8:16], AF.Exp, [PSb[1]], [bsm[p]], scale=-1.0)
                CP(sm[0:64, 28:30], sm[0:64, 24:28:2], [bsm[p]], [bsm[p]])
                CP(sm[64:128, 28:30], sm[64:128, 25:28:2], [bsm[p]], [bsm[p]])
                TT(VBE[p][:, :, 0:64], PS[pvo][:, 0:256].rearrange("p (h d) -> p h d", h=4),
                   sm[:, 16:20].unsqueeze(2).to_broadcast([128, 4, 64]), ALU.mult, [PSb[pvo], bsm[p]], [bvbe[p]])
                CP(VBE[p][:, :, 64:65], sm[:, 16:20].unsqueeze(2), [bsm[p]], [bvbe[p]])
                ACT(OG[:], PS[pvo][:, 256:512], AF.Sigmoid, [PSb[pvo]], [bog])
                TT(OG[:].rearrange("p (h d) -> p h d", h=4), OG[:].rearrange("p (h d) -> p h d", h=4),
                   MNG[:, l, :].unsqueeze(1).to_broadcast([128, 4, 64]), ALU.mult, [bog, PRMb], [bog])
                pk = pbf(2)
                for j in range(2):
                    TR(pk[:, j * 128:(j + 1) * 128], QKT[:, 2 + j, cs], [bqk[2 + j]], [PSb[2]])
                CP(KTK[p][:], pk[:, 0:256], [PSb[2]], [bkt[p]], eng="act")
                CP(QZ[p][0:64, 0:4:2, :], QKT[0:64, 0:2, cs], [bqk[0], bqk[1]], [bqz[p]])
                CP(QZ[p][64:128, 1:4:2, :], QKT[64:128, 0:2, cs], [bqk[0], bqk[1]], [bqz[p]])
                for h in range(4):
                    MM(PS[3][:, h * 128:(h + 1) * 128], QKT[:, 2 + h // 2, cs], QZ[p][:, h, :], True, True,
                       [bqk[2 + h // 2], bqz[p]], [PSb[3]])
                TT(SMT[p][:], PS[3][:].rearrange("p (h t) -> p h t", h=4),
                   MSKB[:].unsqueeze(1).to_broadcast([128, 4, 128]), ALU.mult, [PSb[3], CBb], [bsmt[p]])
                pn = PS[4][:, 0:260].rearrange("p (h e) -> p h e", h=4)
                for h in range(4):
                    hs = slice((h % 2) * 64, (h % 2) * 64 + 64)
                    MM(pn[:, h, :], SMT[p][:, h, :], VBE[p][:, h, :], True, False, [bsmt[p], bvbe[p]], [PSb[4]])
                    MM(pn[:, h, :], QZ[p][:, h, :], CTB[:, h // 2, :], False, True, [bqz[p], bctb], [PSb[4]])
                pc = PS[2][:, 128:388].rearrange("p (a e) -> p a e", a=2)
                for hp in range(2):
                    MM(pc[:, hp, :], KTK[p][:, hp * 128:(hp + 1) * 128],
                       VBE[p][:, 2 * hp:2 * hp + 2, :].rearrange("p a e -> p (a e)"), True, True,
                       [bkt[p], bvbe[p]], [PSb[2]])
                ebl = sm[:, 28:30].unsqueeze(2).to_broadcast([128, 2, 65])
                for hl in range(2):
                    hs = slice(hl * 64, hl * 64 + 64)
                    TT(TC[hs, :, :], pc[hs, :, hl * 65:(hl + 1) * 65], sm[hs, 28:30].unsqueeze(2).to_broadcast([64, 2, 65]),
                       ALU.mult, [PSb[2], bsm[p]], [btc])
                TT(CTF[:], CTF[:], ebl, ALU.mult, [bct, bsm[p]], [bct])
                TT(CTF[:], CTF[:], TC[:], ALU.add, [bct, btc], [bct])
                TT(sm[:, 32:36], pn[:, :, 64:65].rearrange("p h e -> p (h e)"), sm[:, 20:24], ALU.mult, [PSb[4], bsm[p]], [bsm[p]])
                TS(sm[:, 44:48], sm[:, 32:36], -1.0, ALU.mult, [bsm[p]], [bsm[p]], s2=1.0, op1=ALU.max)
                TT(sm[:, 32:36], sm[:, 32:36], sm[:, 44:48], ALU.max, [bsm[p]], [bsm[p]])
                RECIP(sm[:, 32:36], sm[:, 32:36], [bsm[p]], [bsm[p]])
                TT(sm[:, 36:40], sm[:, 32:36], sm[:, 20:24], ALU.mult, [bsm[p]], [bsm[p]])
                TT(HH[:].rearrange("p (h d) -> p h d", h=4), pn[:, :, 0:64],
                   sm[:, 36:40].unsqueeze(2).to_broadcast([128, 4, 64]), ALU.mult, [PSb[4], bsm[p]], [bhh])
                CP(CTB[:], CTF[:], [bct], [bctb])
                ACT(SQ[:], HH[:], AF.Square, [bhh], [bhh])
                RED(sm[:, 40:44], SQ[:].rearrange("p (h d) -> p h d", h=4), [bhh], [bsm[p]])
                RSTD(sm[:, 40:44], sm[:, 40:44], 1.0 / 64, [bsm[p], CBb], [bsm[p]])
                TT(HH[:].rearrange("p (h d) -> p h d", h=4), HH[:].rearrange("p (h d) -> p h d", h=4),
                   sm[:, 40:44].unsqueeze(2).to_broadcast([128, 4, 64]), ALU.mult, [bhh, bsm[p]], [bhh])
                TT(YB[p][:], HH[:], OG[:], ALU.mult, [bhh, bog], [byb[p]])
                py = pbf(3 if False else 6)
                for j in range(2):
                    TR(py[:, j * 128:(j + 1) * 128], YB[p][:, j * 128:(j + 1) * 128], [byb[p]], [PSb[6]])
                CP(MIXT[:, 0:2, cs], py[:, 0:256].rearrange("p (j t) -> p j t", j=2), [PSb[6]], [MXb[c]], eng="act")

        class Pipe:
            def __init__(self):
                self.q, self.t = {}, 0

            def at(self, d, fn, tag=None):
                if FLAT:
                    d = 0
                self.q.setdefault(self.t + d, []).append((fn, tag))

            def tick(self):
                for fn, _ in self.q.pop(self.t, []):
                    fn()
                self.t += 1

            def force(self, pred):
                for t in sorted(self.q):
                    keep = []
                    for fn, tag in self.q[t]:
                        if tag is not None and pred(tag):
                            fn()
                        else:
                            keep.append((fn, tag))
                    self.q[t] = keep
                for t in [t for t in self.q if not self.q[t]]:
                    del self.q[t]

            def flush(self):
                while self.q:
                    self.tick()

        def attn_phase(l):
            KT = sb("AKT", [128, T], BF16, at=PW0)
            QZ = sb("AQZ", [128, 2, T], BF16, at=PW0 + 2 * T)
            VX = sb("AVX", [128, NT, 129], BF16, at=PW0 + 6 * T)
            for h in range(4):
                bs = new_phase(50)
                it_ = iter(bs)
                nb = lambda n: [next(it_) for _ in range(n)]
                bkt, bqz0, bqz1, bvx, bjk = nb(5)
                bsq, bss, bqr, bt1, bt2, bqb = nb(2), nb(4), nb(6), nb(2), nb(2), nb(3)
                bpt, bsml, bto, boo, bya = nb(6), nb(3), nb(2), nb(2), nb(2)
                o = PW0 + 6 * T
                o += (NT * 258 + 31) // 32 * 32
                SQ = [sb("ASQ%d" % i, [128, 256], F32, at=o + 1024 * i) for i in range(2)]
                o += 2048
                QR = [sb("AQR%d" % i, [128, 4, 64], F32, at=o + 1024 * i) for i in range(6)]
                o += 6144
                T1 = [sb("AT1%d" % i, [128, 4, 16], F32, at=o + 256 * i) for i in range(2)]
                o += 512
                T2 = [sb("AT2%d" % i, [128, 4, 16], F32, at=o + 256 * i) for i in range(2)]
                o += 512
                SS = [sb("ASS%d" % i, [128, 8], F32, at=o + 32 * i) for i in range(4)]
                o += 128
                QB = [sb("AQB%d" % i, [128, 256], BF16, at=o + 512 * i) for i in range(3)]
                o += 1536
                PT = [sb("APT%d" % i, [128, 512], BF16, at=o + 1024 * i) for i in range(6)]
                o += 6144
                SML = [sb("ASM%d" % i, [128, 8], F32, at=o + 32 * i) for i in range(3)]
                o += 96
                PO = [sb("APO%d" % i, [128, 2, 129], F32, at=o + 1056 * i) for i in range(2)]
                o += 2112
                OO = [sb("AOO%d" % i, [128, 128], F32, at=o + 512 * i) for i in range(2)]
                o += 1024
                YA = [sb("AYA%d" % i, [128, 128], BF16, at=o + 256 * i) for i in range(2)]
                o += 512
                JK = sb("AJK", [128, 128], F32, at=o)
                o += 512
                assert o <= PWEND, (o, PWEND)
                if h == 0:
                    S.op("dve", lambda e, QZ=QZ: e.memset(QZ[:], 0.0), [], [bqz0, bqz1])
                    S.op("dve", lambda e, VX=VX: e.memset(VX[:, :, 128:129], 1.0), [], [bvx])
                Wa, bwa = WLOAD([(lambda t, j=j: t[:, 0:3072].rearrange("p (k n) -> p k n", k=8)[:, :, j * 128:(j + 1) * 128],
                                  wsrc_in(l, 1032 + 512 * j + h * 128, 128)) for j in range(3)])
                Wav = Wa[:, 0:3072].rearrange("p (k n) -> p k n", k=8)
                pp = Pipe()
                for i in range(NT):
                    ts_ = slice(i * 128, (i + 1) * 128)
                    pa = i % 2
                    qr, sq, ss, t1, t2, qbt = QR[i % 6], SQ[i % 2], SS[i % 4], T1[i % 2], T2[i % 2], QB[i % 3]
                    b_qr, b_sq, b_ss, b_t1, b_t2, b_qb = bqr[i % 6], bsq[i % 2], bss[i % 4], bt1[i % 2], bt2[i % 2], bqb[i % 3]
                    ptb = 2 + i % 2
                    qb3 = qbt[:].rearrange("p (a d) -> p a d", a=4)

                    def s0(i=i, ts_=ts_, pa=pa):
                        for k in range(8):
                            MM(PS[pa][:, 0:384], HT[:, k, ts_], Wav[:, k, :], k == 0, k == 7, [HTb[i], bwa], [PSb[pa]])

                    def s1(i=i, pa=pa, qr=qr, b_qr=b_qr):
                        CP(VX[:, i, 0:128], PS[pa][:, 256:384], [PSb[pa]], [bvx], eng="act")
                        CP(qr[:].rearrange("p a d -> p (a d)"), PS[pa][:, 0:256], [PSb[pa]], [b_qr], eng="act")

                    def s1b(sq=sq, qr=qr, b_sq=b_sq, b_qr=b_qr):
                        ACT(sq[:], qr[:].rearrange("p a d -> p (a d)"), AF.Square, [b_qr], [b_sq])

                    def s2(sq=sq, ss=ss, b_sq=b_sq, b_ss=b_ss):
                        RED(ss[:, 0:4], sq[:].rearrange("p (a d) -> p a d", a=4), [b_sq], [b_ss])

                    def s3(ss=ss, b_ss=b_ss):
                        ACT(ss[:, 0:4], ss[:, 0:4], AF.Sqrt, [b_ss, CBb], [b_ss], bias=EPSt[:], scale=1.0 / 64)

                    def s4(i=i, ss=ss, qr=qr, t1=t1, t2=t2, qb3=qb3, b_ss=b_ss, b_qr=b_qr, b_t1=b_t1, b_t2=b_t2, b_qb=b_qb):
                        RECIP(ss[:, 0:4], ss[:, 0:4], [b_ss], [b_ss])
                        TT(qr[:], qr[:], ss[:, 0:4].unsqueeze(2).to_broadcast([128, 4, 64]), ALU.mult, [b_qr, b_ss], [b_qr])
                        TT(qr[:], qr[:], GQK[:, l, :, :], ALU.mult, [b_qr, PRMb], [b_qr])
                        TT(t1[:], qr[:, :, 0:16], ROPE[:, i, 0:16].unsqueeze(1).to_broadcast([128, 4, 16]), ALU.mult,
                           [b_qr, ROPEb], [b_t1])
                        TT(t2[:, :, 0:8], qr[:, :, 8:16], ROPE[:, i, 16:24].unsqueeze(1).to_broadcast([128, 4, 8]), ALU.mult,
                           [b_qr, ROPEb], [b_t2])
                        TT(t2[:, :, 8:16], qr[:, :, 0:8], ROPE[:, i, 24:32].unsqueeze(1).to_broadcast([128, 4, 8]), ALU.mult,
                           [b_qr, ROPEb], [b_t2])
                        TT(qb3[:, :, 0:16], t1[:], t2[:], ALU.add, [b_t1, b_t2], [b_qb])

                    def s5(qr=qr, qb3=qb3, b_qr=b_qr, b_qb=b_qb):
                        CP(qb3[:, :, 16:64], qr[:, :, 16:64], [b_qr], [b_qb], eng="act")

                    def s6(qbt=qbt, b_qb=b_qb, ptb=ptb):
                        pt = pbf(ptb)
                        for j in range(2):
                            TR(pt[:, j * 128:(j + 1) * 128], qbt[:, j * 128:(j + 1) * 128], [b_qb], [PSb[ptb]])

                    def s7(ts_=ts_, ptb=ptb):
                        pt = pbf(ptb)
                        CP(KT[:, ts_], pt[:, 128:256], [PSb[ptb]], [bkt], eng="act")
                        CP(QZ[0:64, 0, ts_], pt[0:64, 0:128], [PSb[ptb]], [bqz0], eng="act")
                        CP(QZ[64:128, 1, ts_], pt[64:128, 0:128], [PSb[ptb]], [bqz1], eng="act")

                    for d_, fn_ in enumerate((s0, s1, s1b, s2, s3, s4, s5, s6, s7)):
                        if 30 <= sub < 40 and d_ > sub - 30:
                            continue
                        pp.at(d_, fn_)
                    pp.tick()
                pp.flush()
                if sub == 3 or 30 <= sub < 40:
                    continue
                groups = []
                for qb in range(NT):
                    ng = (qb + 2) // 2
                    for g in range(ng):
                        groups.append((qb, [kb for kb in (2 * g, 2 * g + 1) if kb <= qb], g == ng - 1))
                pp = Pipe()
                for n, (qb, kbs, last) in enumerate(groups):
                    sbk, pbk, p = n % 3, n % 6, qb % 2
                    qs_ = slice(qb * 128, (qb + 1) * 128)
                    nj = len(kbs)

                    def gS(kbs=kbs, sbk=sbk, qs_=qs_):
                        for j, kb in enumerate(kbs):
                            MM(PS[sbk][:, j * 256:(j + 1) * 256].rearrange("p (c t) -> p c t", c=2),
                               KT[:, kb * 128:(kb + 1) * 128], QZ[:, :, qs_], True, True, [bkt, bqz0, bqz1], [PSb[sbk]])

                    def gE(sbk=sbk, pbk=pbk, nj=nj):
                        ACT(PT[pbk][:, 0:nj * 256], PS[sbk][:, 0:nj * 256], AF.Exp, [PSb[sbk]], [bpt[pbk]])

                    def gM(pbk=pbk, nj=nj):
                        ptv = PT[pbk][:, (nj - 1) * 256:nj * 256].rearrange("p (c t) -> p c t", c=2)
                        TT(ptv, ptv, MSKB[:].unsqueeze(1).to_broadcast([128, 2, 128]), ALU.mult, [bpt[pbk], CBb], [bpt[pbk]])

                    def gV(kbs=kbs, pbk=pbk, p=p, qb=qb):
                        if kbs[0] == 0:
                            pp.force(lambda tg: tg <= qb - 2)
                        for c in range(2):
                            bk = 3 + 2 * p + c
                            for j, kb in enumerate(kbs):
                                MM(PS[bk][:, 0:129], PT[pbk][:, j * 256 + c * 128:j * 256 + (c + 1) * 128], VX[:, kb, :],
                                   kb == 0, kb == qb, [bpt[pbk], bvx], [PSb[bk]])

                    pp.at(0, gS)
                    pp.at(2, gE)
                    if last:
                        pp.at(4, gM)
                    pp.at(7, gV)
                    if last:
                        b0, b1 = 3 + 2 * p, 4 + 2 * p
                        sml, b_sml = SML[qb % 3], bsml[qb % 3]

                        def e1(b0=b0, b1=b1, p=p, qb=qb):
                            pp.force(lambda tg: tg <= qb - 2)
                            CP(PO[p][:, 0, :], PS[b0][:, 0:129], [PSb[b0]], [bto[p]], eng="act")
                            CP(PO[p][:, 1, :], PS[b1][:, 0:129], [PSb[b1]], [bto[p]], eng="act")

                        def e2(p=p, sml=sml, b_sml=b_sml):
                            RECIP(sml[:, 0:2], PO[p][:, :, 128:129].rearrange("p c e -> p (c e)"), [bto[p]], [b_sml])
                            TT(sml[:, 2:3], sml[:, 1:2], NLAM[:, l, 0:1], ALU.mult, [b_sml, PRMb], [b_sml])
                            TS(PO[p][:, 1, 0:128], PO[p][:, 1, 0:128], sml[:, 2:3], ALU.mult, [bto[p], b_sml], [bto[p]])

                        def e3(p=p, sml=sml, b_sml=b_sml):
                            STT(OO[p][:], PO[p][:, 0, 0:128], sml[:, 0:1], PO[p][:, 1, 0:128], ALU.mult, ALU.add,
                                [bto[p], b_sml], [boo[p]])

                        def e4(p=p, sml=sml, b_sml=b_sml):
                            ACT(JK[:], OO[p][:], AF.Square, [boo[p]], [bjk, b_sml], accum=sml[:, 3:4])
                            ACT(sml[:, 3:4], sml[:, 3:4], AF.Sqrt, [b_sml, CBb], [b_sml], bias=EPSt[:], scale=1.0 / 128)

                        def e5(p=p, sml=sml, b_sml=b_sml):
                            RECIP(sml[:, 3:4], sml[:, 3:4], [b_sml], [b_sml])
                            STT(YA[p][:], OO[p][:], sml[:, 3:4], SLG[:, l, :], ALU.mult, ALU.mult, [boo[p], b_sml, PRMb], [bya[p]])

                        def e6(p=p):
                            TR(pbf(7)[:, p * 128:(p + 1) * 128], YA[p][:], [bya[p]], [PSb[7]])

                        def e7(qb=qb, qs_=qs_, p=p):
                            CP(MIXT[:, 2 + h, qs_], pbf(7)[:, p * 128:(p + 1) * 128], [PSb[7]], [MXb[qb]], eng="act")

                        for d_, fn_ in ((9, e1), (11, e2), (13, e3), (15, e4), (18, e5), (20, e6), (22, e7)):
                            pp.at(d_, fn_, tag=qb)
                    pp.tick()
                pp.flush()

        def hgrn_phase(l):
            bs = new_phase(26)
            (bqs, bsg, bf, blf, bkk, beq, bek, besc, bsf, _u, bts, bamt, bsq, bgg, bhh, bss) = bs[0:16]
            bqh, bvh, bqkt, byb, bsb = bs[16:18], bs[18:20], bs[20:22], bs[22:24], bs[24:26]
            o = PW0
            f32t = lambda name, i: sb(name, [128, 256], F32, at=o + 1024 * i)
            QS, SGt, Ft, LF, KK, EQ, EK, SQ, GG, HHt = [f32t("H%d" % i, i) for i in range(10)]
            o2 = o + 10240
            QH = [sb("HQH%d" % i, [128, 512], BF16, at=o2 + 1024 * i) for i in range(2)]
            VH = [sb("HVH%d" % i, [128, 256], BF16, at=o2 + 2048 + 512 * i) for i in range(2)]
            QKHT = [sb("HQKT%d" % i, [64, 8, 128], BF16, at=o2 + 3072 + 2048 * i) for i in range(2)]
            ESC = sb("HESC", [64, 4, 8], F32, at=o2 + 7168)
            SF = sb("HSF", [64, 4, 64], F32, at=o2 + 7296)
            SBa = [sb("HSB%d" % i, [64, 4, 64], BF16, at=o2 + 8320 + 11936 - 8320 + 512 * i) for i in range(2)]
            TS_ = sb("HTS", [64, 4, 64], F32, at=o2 + 8832)
            AMT = sb("HAMT", [128, 4, 128], BF16, at=o2 + 9856)
            SSm = sb("HSS", [128, 8], F32, at=o2 + 10880)
            YB = [sb("HYB%d" % i, [128, 256], BF16, at=o2 + 10912 + 512 * i) for i in range(2)]
            LFH = sb("HLFH", [128, 512], BF16, at=o2 + 12960)
            blfh = Buf()
            S.alias([blfh], bs)
            pw_bufs.append(blfh)
            W1, bw1 = WLOAD([(wview(512), wsrc_in(l, 2568, 512))])
            W2, bw2 = WLOAD([(wview(512), wsrc_in(l, 3080, 512))])
            W1v, W2v = wview(512)(W1), wview(512)(W2)
            S.op("dve", lambda e: e.memset(SF[:], 0.0), [], [bsf])
            for i in range(NT):
                p = i % 2
                ts_ = slice(i * 128, (i + 1) * 128)
                for k in range(8):
                    MM(PS[0][:], HT[:, k, ts_], W1v[:, k, :], k == 0, k == 7, [HTb[i], bw1], [PSb[0]])
                for k in range(8):
                    MM(PS[1][:], HT[:, k, ts_], W2v[:, k, :], k == 0, k == 7, [HTb[i], bw2], [PSb[1]])
                ACT(QS[:], PS[0][:, 0:256], AF.Silu, [PSb[0]], [bqs])
                ACT(SGt[:], PS[0][:, 256:512], AF.Sigmoid, [PSb[0]], [bsg])
                TT(Ft[:], SGt[:], OMLB[:, l, :], ALU.mult, [bsg, PRMb], [bf])
                TT(Ft[:], Ft[:], LBB[:, l, :], ALU.add, [bf, PRMb], [bf])
                ACT(LF[:], Ft[:], AF.Ln, [bf], [blf])
                TS(KK[:], Ft[:], -1.0, ALU.mult, [bf], [bkk], s2=1.0, op1=ALU.add)
                CP(VH[p][:], PS[1][:, 0:256], [PSb[1]], [bvh[p]], eng="act")
                ACT(GG[:], PS[1][:, 256:512], AF.Silu, [PSb[1]], [bgg])
                TT(GG[:].rearrange("p (h d) -> p h d", h=4), GG[:].rearrange("p (h d) -> p h d", h=4),
                   HNG[:, l, :].unsqueeze(1).to_broadcast([128, 4, 64]), ALU.mult, [bgg, PRMb], [bgg])
                CP(LFH[:, 0:256], LF[:], [blf], [blfh])
                TT(EQ[:], LF[:], LFH[:, 0:256], ALU.subtract, [blf, blfh, beq], [beq])
                CP(LFH[:, 256:512], EQ[:], [beq], [blfh])
                for q_ in range(2):
                    MM(PS[2][:, 0:256], TSHB[:], LFH[:, 256 * q_:256 * q_ + 256], q_ == 0, q_ == 1, [CBb, blfh], [PSb[2]])
                psc = PS[3][0:64, 0:16].rearrange("p (h e) -> p h e", h=4)
                for hh in range(4):
                    for q_ in range(2):
                        MM(psc[:, hh, :], LFH[:, 256 * q_ + hh * 64:256 * q_ + (hh + 1) * 64], SELB[:], q_ == 0, q_ == 1,
                           [blfh, CBb], [PSb[3]])
                ACT(EQ[:], PS[2][:, 0:256], AF.Exp, [PSb[2]], [beq])
                ACT(EK[:], PS[2][:, 0:256], AF.Exp, [PSb[2]], [bek], scale=-1.0)
                ACT(ESC[:, :, 0:4], psc, AF.Exp, [PSb[3]], [besc])
                TT(ESC[:, :, 4:6], ESC[:, :, 0:4:2], ESC[:, :, 1:4:2], ALU.mult, [besc], [besc])
                TT(QH[p][:, 0:256], QS[:], EQ[:], ALU.mult, [bqs, beq], [bqh[p]])
                TT(QH[p][:, 256:512], KK[:], EK[:], ALU.mult, [bkk, bek], [bqh[p]])
                pt = pbf(4)[0:64, :]
                for j in range(8):
                    TR(pt[:, j * 128:(j + 1) * 128], QH[p][:, j * 64:(j + 1) * 64], [bqh[p]], [PSb[4]])
                CP(QKHT[p][:], pt.rearrange("p (j t) -> p j t", j=8), [PSb[4]], [bqkt[p]])
                for hh in range(4):
                    MM(PS[5][:, hh * 128:(hh + 1) * 128], QKHT[p][:, 4 + hh, :], QKHT[p][:, hh, :], True, True,
                       [bqkt[p]], [PSb[5]])
                TT(AMT[:], PS[5][:].rearrange("p (h t) -> p h t", h=4), MSKHB[:].unsqueeze(1).to_broadcast([128, 4, 128]),
                   ALU.mult, [PSb[5], CBb], [bamt])
                po = PS[6][:, 0:256].rearrange("p (h d) -> p h d", h=4)
                for a in range(2):
                    asl = slice(a * 64, a * 64 + 64)
                    TT(SBa[a][:], SF[:], ESC[:, :, 2 * a:2 * a + 1].to_broadcast([64, 4, 64]), ALU.mult, [bsf, besc], [bsb[a]])
                    ps2 = PS[7][0:64, 0:256].rearrange("p (h d) -> p h d", h=4)
                    for hh in range(4):
                        MM(ps2[:, hh, :], QH[p][asl, 256 + hh * 64:256 + (hh + 1) * 64], VH[p][asl, hh * 64:(hh + 1) * 64],
                           True, True, [bqh[p], bvh[p]], [PSb[7]])
                    TT(TS_[:], ps2, ESC[:, :, 2 * a + 1:2 * a + 2].to_broadcast([64, 4, 64]), ALU.mult, [PSb[7], besc], [bts])
                    TT(SF[:], SF[:], ESC[:, :, 4 + a:5 + a].to_broadcast([64, 4, 64]), ALU.mult, [bsf, besc], [bsf])
                    TT(SF[:], SF[:], TS_[:], ALU.add, [bsf, bts], [bsf])
                for hh in range(4):
                    MM(po[:, hh, :], AMT[:, hh, :], VH[p][:, hh * 64:(hh + 1) * 64], True, False, [bamt, bvh[p]], [PSb[6]])
                    for a in range(2):
                        asl = slice(a * 64, a * 64 + 64)
                        MM(po[asl, hh, :], QKHT[p][:, hh, asl], SBa[a][:, hh, :], False, True, [bqkt[p], bsb[a]], [PSb[6]])
                ACT(SQ[:], PS[6][:, 0:256], AF.Square, [PSb[6]], [bsq])
                RED(SSm[:, 0:4], SQ[:].rearrange("p (h d) -> p h d", h=4), [bsq], [bss])
                RSTD(SSm[:, 0:4], SSm[:, 0:4], 1.0 / 64, [bss, CBb], [bss])
                TT(HHt[:].rearrange("p (h d) -> p h d", h=4), po, SSm[:, 0:4].unsqueeze(2).to_broadcast([128, 4, 64]),
                   ALU.mult, [PSb[6], bss], [bhh])
                TT(YB[p][:], HHt[:], GG[:], ALU.mult, [bhh, bgg], [byb[p]])
                py = pbf(3)
                for j in range(2):
                    TR(py[:, 512 + j * 128:512 + (j + 1) * 128], YB[p][:, j * 128:(j + 1) * 128], [byb[p]], [PSb[3]])
                CP(MIXT[:, 6:8, ts_], py[:, 512:768].rearrange("p (j t) -> p j t", j=2), [PSb[3]], [MXb[i]], eng="act")

        def outproj_phase(l):
            ws = []
            for nh in range(2):
                ws.append(WLOAD([(wview(512), w_out[l, 0:D, nh * 512:(nh + 1) * 512].rearrange("(k p) n -> p k n", p=128))]))
            it = 0
            for i in range(NT):
                for nh in range(2):
                    b_ = it % 4
                    it += 1
                    wv_ = wview(512)(ws[nh][0])
                    for k in range(8):
                        MM(PS[b_][:], MIXT[:, k, i * 128:(i + 1) * 128], wv_[:, k, :], k == 0, k == 7,
                           [MXb[i], ws[nh][1]], [PSb[b_]])
                    TT(X[:, i, nh * 512:(nh + 1) * 512], PS[b_][:], X[:, i, nh * 512:(nh + 1) * 512], ALU.add,
                       [PSb[b_], Xb[i]], [Xb[i]])

        def ffn_phase(l):
            bs = new_phase(2)
            br = bs
            R = [sb("FR%d" % i, [128, 512], F32, at=PW0 + 8192 + 2048 * i) for i in range(2)]
            if T >= 2048:
                AT = MIXT[:].rearrange("p k t -> p (k t)")
                ATv = AT[:, 0:32 * 512].rearrange("p (j t) -> p j t", j=32)
            else:
                ATv = sb("ATs", [128, 32, 512], BF16, at=PW0 + 36000)[:]
            bat = [Buf() for _ in range(32)]
            it = 0
            for G in range(NG):
                S.alias(bat, MXb + bat)
                for jg in range(8):
                    wu, bwu = WLOAD([(wview(512), w_up[l, 0:D, jg * 512:(jg + 1) * 512].rearrange("(k p) n -> p k n", p=128))])
                    wuv = wview(512)(wu)
                    for jj in range(4):
                        j = jg * 4 + jj
                        p = it % 2
                        it += 1
                        for k in range(8):
                            MM(PS[p][:], wuv[:, k, jj * 128:(jj + 1) * 128], HT[:, k, G * 512:(G + 1) * 512], k == 0, k == 7,
                               [bwu] + HTb[4 * G:4 * G + 4], [PSb[p]])
                        ACT(R[p][:], PS[p][:], AF.Relu, [PSb[p]], [br[p]])
                        TT(ATv[:, j, :], R[p][:], R[p][:], ALU.mult, [br[p]], [bat[j]])
                for nh in range(2):
                    for jg in range(4):
                        wd, bwd = WLOAD([(wview(512), w_dn[l, jg * 1024:(jg + 1) * 1024, nh * 512:(nh + 1) * 512]
                                          .rearrange("(j p) n -> p j n", p=128))])
                        wdv = wview(512)(wd)
                        for ti in range(4):
                            for jj in range(8):
                                j = jg * 8 + jj
                                MM(PS[2 + ti][:], ATv[:, j, ti * 128:(ti + 1) * 128], wdv[:, jj, :], j == 0, j == 31,
                                   [bat[j], bwd], [PSb[2 + ti]])
                    for ti in range(4):
                        i = G * 4 + ti
                        TT(X[:, i, nh * 512:(nh + 1) * 512], PS[2 + ti][:], X[:, i, nh * 512:(nh + 1) * 512], ALU.add,
                           [PSb[2 + ti], Xb[i]], [Xb[i]])
            S.alias(MXb, bat)

        for s in range(NSEQ):
            for i in range(NT):
                S.dma("sp", lambda e, s=s, i=i: e.dma_start(out=X[:, i, :], in_=x_d[s * T + i * 128:s * T + (i + 1) * 128, :]),
                      "ldx%d" % i, writes=[Xb[i]])
            for l in range(DEPTH):
                if stop >= 1:
                    norm_phase(G1C, l)
                if stop >= 2:
                    mlstm_phase(l)
                if stop >= 3:
                    attn_phase(l)
                if stop >= 4:
                    hgrn_phase(l)
                if stop < 5:
                    continue
                if dbg and s == 0 and l == 0:
                    S.dma("sp", lambda e: e.dma_start(out=dbg_d[:, :, :], in_=MIXT[:]), "std", reads=MXb)
                outproj_phase(l)
                norm_phase(G2C, l)
                ffn_phase(l)
            for i in range(NT):
                S.dma("sp", lambda e, s=s, i=i: e.dma_start(out=y_d[s * T + i * 128:s * T + (i + 1) * 128, :], in_=X[:, i, :]),
                      "sty%d" % i, reads=[Xb[i]])
        S.wait_all("sp", Xb + MXb)
        global _LAST_S
        _LAST_S = S
        S.emit()
    return nc


_CACHE = {}


def kernel(**inputs):
    x = np.ascontiguousarray(inputs["x"], dtype=np.float32)
    B, T, _ = x.shape
    NCORE = 8
    NSEQ = B // NCORE
    DEPTH = inputs["w_in"].shape[0]
    key = (T, NSEQ, DEPTH)
    if key not in _CACHE:
        _CACHE[key] = build(T, NSEQ, DEPTH)
    nc = _CACHE[key]
    cst, rope = host_consts(T)
    shared = {k: np.ascontiguousarray(v, dtype=np.float32) for k, v in inputs.items() if k != "x"}
    shared["cst"] = cst
    shared["rope"] = rope
    in_maps = []
    for c in range(NCORE):
        m = dict(shared)
        for k in ("w_in", "w_out", "w_up", "w_down"):
            w = shared[k]
            wp = np.empty((w.shape[0], w.shape[1] + 1, w.shape[2]), np.float32)
            wp[:, :-1] = w
            wp[:, -1] = float(c)
            m[k] = wp
        m["x"] = x[c * NSEQ:(c + 1) * NSEQ].reshape(NSEQ * T, D)
        in_maps.append(m)
    res = run_bass_kernel_spmd(nc, in_maps, core_ids=list(range(NCORE)))
    out = np.concatenate([r["y"].reshape(NSEQ, T, D) for r in res.results], axis=0)
    return out.astype(np.float32)
```

```python
import math
import numpy as np
from contextlib import ExitStack
import concourse.bass as bass
import concourse.mybir as mybir
from concourse.bass_utils import run_bass_kernel_spmd

F32 = mybir.dt.float32
BF16 = mybir.dt.bfloat16
AF = mybir.ActivationFunctionType
ALU = mybir.AluOpType
AX = mybir.AxisListType
D = 1024
DIN = 3592
DFF = 4096
EPS = 1e-6
ENGS = ("pe", "act", "dve", "pool", "sp")
import os
FLAT = bool(int(os.environ.get('PIPE_FLAT', '0')))


class Buf:
    __slots__ = ("lw", "rd")

    def __init__(self):
        self.lw = None
        self.rd = {}


class Sched:
    def __init__(self, nc, stack):
        self.nc, self.stack = nc, stack
        self.prog = {e: [] for e in ENGS}
        self.sems, self.cnt, self.waited = {}, {}, {}
        for e in ENGS:
            self.new_sem(e)

    def new_sem(self, key):
        self.sems[key] = self.stack.enter_context(self.nc.semaphore("s_" + str(key)))
        self.cnt[key] = 0
        return key

    def _deps(self, reads, writes):
        deps = {}
        for b in reads:
            if b.lw is not None and deps.get(b.lw[0], 0) < b.lw[1]:
                deps[b.lw[0]] = b.lw[1]
        for b in writes:
            if b.lw is not None and deps.get(b.lw[0], 0) < b.lw[1]:
                deps[b.lw[0]] = b.lw[1]
            for k, v in b.rd.items():
                if deps.get(k, 0) < v:
                    deps[k] = v
        return deps

    def _waits(self, eng, deps):
        ws = []
        for k, v in deps.items():
            if self.waited.get((eng, k), 0) < v:
                self.waited[(eng, k)] = v
                ws.append((k, v))
                if k in self.prog:
                    self.prog[k][v - 1][3] = True
        return ws

    def op(self, eng, fn, reads=(), writes=()):
        if getattr(self, "cap", None) is not None:
            self.cap.append((eng, fn, tuple(reads), tuple(writes)))
            return
        deps = self._deps(reads, writes)
        if eng == "pe":
            deps.pop("pe", None)
        ws = self._waits(eng, deps)
        self.prog[eng].append([ws, fn, eng, False])
        n = len(self.prog[eng])
        for b in reads:
            b.rd[eng] = n
        for b in writes:
            b.lw = (eng, n)
            b.rd = {}

    def dma(self, queue, fns, semkey, reads=(), writes=()):
        if not isinstance(fns, (list, tuple)):
            fns = [fns]
        deps = self._deps(reads, writes)
        ws = self._waits(queue, deps)
        for j, fn in enumerate(fns):
            self.cnt[semkey] += 16
            self.prog[queue].append([ws if j == 0 else [], fn, semkey, True])
        n = self.cnt[semkey]
        for b in reads:
            b.rd[semkey] = n
        for b in writes:
            b.lw = (semkey, n)
            b.rd = {}

    def alias(self, new_bufs, old_bufs):
        m = {}
        for b in old_bufs:
            if b.lw is not None and m.get(b.lw[0], 0) < b.lw[1]:
                m[b.lw[0]] = b.lw[1]
            for k, v in b.rd.items():
                if m.get(k, 0) < v:
                    m[k] = v
        for b in new_bufs:
            b.lw = None
            b.rd = dict(m)

    def wait_all(self, eng, bufs):
        deps = self._deps(bufs, bufs)
        ws = self._waits(eng, deps)
        self.prog[eng].append([ws, None, None, False])

    def emit(self):
        nc = self.nc
        names = {"pe": "tensor", "act": "scalar", "dve": "vector", "pool": "gpsimd", "sp": "sync"}
        pref = {}
        for e in ENGS:
            c, arr = 0, []
            for ent in self.prog[e]:
                if ent[3] and ent[2] == e:
                    c += 1
                arr.append(c)
            pref[e] = arr
        sems = self.sems
        with nc.Block() as block:
            for e in ENGS:
                def body(eng, e=e):
                    for ws, fn, inckey, flagged in self.prog[e]:
                        for k, v in ws:
                            eng.wait_ge(sems[k], pref[k][v - 1] if k in pref else v)
                        if fn is not None:
                            ins = fn(eng)
                            if inckey == e:
                                if flagged:
                                    ins.then_inc(sems[e], 1)
                            else:
                                ins.then_inc(sems[inckey], 16)
                getattr(block, names[e])(body)


def host_consts(T):
    NT = T // 128
    c = np.zeros((128, 648), np.float32)
    i = np.arange(128)
    c[:, 0:128] = np.eye(128)
    c[:, 128:256] = (i[:, None] <= i[None, :])
    c[:, 256:384] = 1.0
    sl, tl = i[:, None] % 64, i[None, :] % 64
    same = (i[:, None] // 64) == (i[None, :] // 64)
    c[:, 384:512] = same * ((sl <= tl).astype(np.float32) - (sl <= 31))
    c[:, 512:640] = same * (sl <= tl)
    for a in range(2):
        c[:, 640 + 2 * a] = ((i // 64) == a) & ((i % 64) <= 31)
        c[:, 641 + 2 * a] = ((i // 64) == a) & ((i % 64) > 31)
    half = 8
    inv = (500000.0 ** (-np.arange(half, dtype=np.float32) / half)).astype(np.float32)
    ang = np.arange(T, dtype=np.float32)[:, None] * inv[None, :]
    cos, sin = np.cos(ang).astype(np.float32), np.sin(ang).astype(np.float32)
    tab = np.concatenate([cos, cos, -sin, sin], axis=1)
    rope = np.ascontiguousarray(tab.reshape(NT, 128, 32).transpose(1, 0, 2))
    return c, rope


def build(T=2048, NSEQ=2, DEPTH=2, dbg=False, stop=99, sub=0):
    NT = T // 128
    NG = T // 512
    nc = bass.Bass("TRN2", target_bir_lowering=False)
    dt_in = lambda name, shape: nc.dram_tensor(name, shape, F32, kind="ExternalInput").ap()
    x_d = dt_in("x", [NSEQ * T, D])
    w_in = dt_in("w_in", [DEPTH, D + 1, DIN])
    w_out = dt_in("w_out", [DEPTH, D + 1, D])
    w_up = dt_in("w_up", [DEPTH, D + 1, DFF])
    w_dn = dt_in("w_down", [DEPTH, DFF + 1, D])
    n1g = dt_in("norm1_g", [DEPTH, D])
    n2g = dt_in("norm2_g", [DEPTH, D])
    convw = dt_in("mlstm_conv_w", [DEPTH, 4, 512])
    gateb = dt_in("mlstm_gate_b", [DEPTH, 2, 4])
    mng = dt_in("mlstm_norm_g", [DEPTH, 64])
    qng = dt_in("diff_q_norm_g", [DEPTH, 64])
    kng = dt_in("diff_k_norm_g", [DEPTH, 64])
    dlam = dt_in("diff_lambda", [DEPTH, 4, 64])
    slg = dt_in("diff_subln_g", [DEPTH, 128])
    lbp = dt_in("hgrn_lb_param", [DEPTH, 256])
    hng = dt_in("hgrn_norm_g", [DEPTH, 64])
    cst_d = dt_in("cst", [128, 648])
    rope_d = dt_in("rope", [128, NT, 32])
    y_d = nc.dram_tensor("y", [NSEQ * T, D], F32, kind="ExternalOutput").ap()
    if dbg:
        dbg_d = nc.dram_tensor("dbg", [128, 8, T], BF16, kind="ExternalOutput").ap()

    with ExitStack() as st:
        S = Sched(nc, st)
        off = [16512]

        def sb(name, shape, dt, at=None):
            n = 1
            for s_ in shape[1:]:
                n *= s_
            nbytes = n * (2 if dt == BF16 else 4)
            nbytes = (nbytes + 31) // 32 * 32
            if at is None:
                at = off[0]
                off[0] += nbytes
            return nc.alloc_sbuf_tensor_at(name, list(shape), dt, offset=at)

        X = sb("X", [128, NT, D], F32)
        Xb = [Buf() for _ in range(NT)]
        HT = sb("HT", [128, 8, T], BF16)
        HTb = [Buf() for _ in range(NT)]
        MIXT = sb("MIXT", [128, 8, T], BF16)
        MXb = [Buf() for _ in range(NT)]
        NS = 3
        RING = [sb("ring%d" % i, [128, 4096], BF16) for i in range(NS)]
        RGb = [Buf() for _ in range(NS)]
        for i in range(NS):
            S.new_sem("w%d" % i)
        CST = sb("CST", [128, 648], F32); CSTb = Buf()
        ROPE = sb("ROPE", [128, NT, 32], F32); ROPEb = Buf()
        IDB = sb("IDB", [128, 128], BF16)
        MSKB = sb("MSKB", [128, 128], BF16)
        MSKHB = sb("MSKHB", [128, 128], BF16)
        ONESB = sb("ONESB", [128, 128], BF16)
        TSHB = sb("TSHB", [128, 128], BF16)
        SELB = sb("SELB", [128, 4], BF16)
        CBb = Buf()
        EPSt = sb("EPSt", [128, 1], F32)
        ONEt = sb("ONEt", [128, 1], F32)
        G1C = sb("G1C", [128, DEPTH, 8], F32)
        G2C = sb("G2C", [128, DEPTH, 8], F32)
        CW = sb("CW", [128, DEPTH, 4, 4], F32)
        GBB = sb("GBB", [128, DEPTH, 8], F32)
        MNG = sb("MNG", [128, DEPTH, 64], F32)
        GQK = sb("GQK", [128, DEPTH, 4, 64], F32)
        SLG = sb("SLG", [128, DEPTH, 128], F32)
        HNG = sb("HNG", [128, DEPTH, 64], F32)
        LBP = sb("LBP", [128, DEPTH, 256], F32)
        LBB = sb("LBB", [128, DEPTH, 256], F32)
        OMLB = sb("OMLB", [128, DEPTH, 256], F32)
        LAMI = sb("LAMI", [128, DEPTH, 4, 64], F32)
        NLAM = sb("NLAM", [128, DEPTH, 2], F32)
        WG = sb("WG", [128, DEPTH, 8, 8], BF16)
        PRMb = Buf()
        S.new_sem("ldc"); S.new_sem("ldr"); S.new_sem("ldp"); S.new_sem("ldg"); S.new_sem("std")
        for i in range(NT):
            S.new_sem("ldx%d" % i); S.new_sem("sty%d" % i)
        WGb = Buf()
        pfns, gfns = [], []
        PW0 = off[0]
        PWEND = 229344
        assert PW0 + 36000 <= PWEND, (PW0, PWEND)

        PS = [nc.alloc_psum_tensor("ps%d" % i, [128, 512], F32) for i in range(8)]
        PSb = [Buf() for _ in range(8)]

        def pbf(i):
            return PS[i][:].bitcast(BF16)

        def ACT(out, in_, func, reads, writes, bias=None, scale=None, accum=None):
            kw = {}
            if bias is not None:
                kw["bias"] = bias
            if scale is not None:
                kw["scale"] = scale
            if accum is not None:
                kw["accum_out"] = accum
            S.op("act", lambda e: e.activation(out=out, in_=in_, func=func, **kw), reads, writes)

        def TT(out, in0, in1, op, reads, writes, eng="dve"):
            S.op(eng, lambda e: e.tensor_tensor(out=out, in0=in0, in1=in1, op=op), reads, writes)

        def TS(out, in0, s1, op0, reads, writes, s2=None, op1=None, eng="dve"):
            if op1 is None:
                S.op(eng, lambda e: e.tensor_scalar(out=out, in0=in0, scalar1=s1, scalar2=None, op0=op0), reads, writes)
            else:
                S.op(eng, lambda e: e.tensor_scalar(out=out, in0=in0, scalar1=s1, scalar2=s2, op0=op0, op1=op1), reads, writes)

        def STT(out, in0, scalar, in1, op0, op1, reads, writes):
            S.op("dve", lambda e: e.scalar_tensor_tensor(out=out, in0=in0, scalar=scalar, in1=in1, op0=op0, op1=op1),
                 reads, writes)

        def CP(out, in_, reads, writes, eng="dve"):
            if eng == "act":
                S.op("act", lambda e: e.copy(out=out, in_=in_), reads, writes)
            else:
                S.op(eng, lambda e: e.tensor_copy(out=out, in_=in_), reads, writes)

        def RECIP(out, in_, reads, writes):
            S.op("dve", lambda e: e.reciprocal(out=out, in_=in_), reads, writes)

        def RED(out, in_, reads, writes):
            S.op("dve", lambda e: e.tensor_reduce(out=out, in_=in_, axis=AX.X, op=ALU.add), reads, writes)

        def MM(out, lhsT, rhs, start, stop, reads, writes):
            S.op("pe", lambda e: e.matmul(out, lhsT=lhsT, rhs=rhs, start=start, stop=stop), reads, writes)

        def TR(out, in_, reads, writes):
            S.op("pe", lambda e: e.transpose(out=out, in_=in_, identity=IDB[:]), list(reads) + [CBb], writes)

        def RSTD(out, in_, scale, reads, writes):
            P = out.shape[0]
            ACT(out, in_, AF.Sqrt, reads, writes, bias=EPSt[0:P, :], scale=scale)
            RECIP(out, out, writes, writes)

        wctr = [0]

        def WLOAD(parts):
            s = wctr[0] % NS
            wctr[0] += 1
            fns = []
            for dst, src in parts:
                for k in range(8):
                    fns.append(lambda e, dst=dst, src=src, k=k: e.dma_start(out=dst(RING[s])[:, k, :], in_=src[:, k, :]))
            S.dma("pool", fns, "w%d" % s, writes=[RGb[s]])
            return RING[s], RGb[s]

        S.dma("sp", lambda e: e.dma_start(out=CST[:], in_=cst_d[:, :]), "ldc", writes=[CSTb])
        S.dma("sp", lambda e: e.dma_start(out=ROPE[:], in_=rope_d[:, :, :]), "ldr", writes=[ROPEb])
        CP(IDB[:], CST[:, 0:128], [CSTb], [CBb])
        CP(MSKB[:], CST[:, 128:256], [CSTb], [CBb])
        CP(MSKHB[:], CST[:, 512:640], [CSTb], [CBb])
        CP(ONESB[:], CST[:, 256:384], [CSTb], [CBb])
        CP(TSHB[:], CST[:, 384:512], [CSTb], [CBb])
        CP(SELB[:], CST[:, 640:644], [CSTb], [CBb])
        S.op("dve", lambda e: e.memset(EPSt[:], EPS), [], [CBb])
        S.op("dve", lambda e: e.memset(ONEt[:], 1.0), [], [CBb])
        for l in range(DEPTH):
            for k in range(8):
                for (dst, src) in ((G1C, n1g), (G2C, n2g)):
                    pfns.append(lambda e, dst=dst, src=src, l=l, k=k: e.dma_start(
                        out=dst[:, l, k:k + 1], in_=src[l:l + 1, k * 128:(k + 1) * 128].rearrange("o p -> p o")))
            for ct in range(4):
                for j in range(4):
                    pfns.append(lambda e, l=l, ct=ct, j=j: e.dma_start(
                        out=CW[:, l, ct, j:j + 1],
                        in_=convw[l, j:j + 1, ct * 128:(ct + 1) * 128].rearrange("o p -> p o")))
            bc = lambda ap, n: ap.to_broadcast([128, n])
            pfns.append(lambda e, l=l: e.dma_start(out=GBB[:, l, :], in_=bc(gateb[l:l + 1].rearrange("o a b -> o (a b)"), 8)))
            pfns.append(lambda e, l=l: e.dma_start(out=MNG[:, l, :], in_=bc(mng[l:l + 1, :], 64)))
            for j in range(2):
                pfns.append(lambda e, l=l, j=j: e.dma_start(out=GQK[:, l, j, :], in_=bc(qng[l:l + 1, :], 64)))
                pfns.append(lambda e, l=l, j=j: e.dma_start(out=GQK[:, l, 2 + j, :], in_=bc(kng[l:l + 1, :], 64)))
            pfns.append(lambda e, l=l: e.dma_start(out=SLG[:, l, :], in_=bc(slg[l:l + 1, :], 128)))
            pfns.append(lambda e, l=l: e.dma_start(out=HNG[:, l, :], in_=bc(hng[l:l + 1, :], 64)))
            pfns.append(lambda e, l=l: e.dma_start(out=LAMI[:, l, :, :].rearrange("p a b -> p (a b)"),
                                                   in_=bc(dlam[l:l + 1].rearrange("o a b -> o (a b)"), 256)))
            pfns.append(lambda e, l=l: e.dma_start(out=LBP[:, l, :], in_=bc(lbp[l:l + 1, :], 256)))
            for k in range(8):
                gfns.append(lambda e, l=l, k=k: e.dma_start(
                    out=WG[:, l, k, :], in_=w_in[l, k * 128:(k + 1) * 128, 1024:1032]))
        S.dma("sp", pfns, "ldp", writes=[PRMb])
        S.dma("pool", gfns, "ldg", writes=[WGb])
        lam_init = [0.8 - 0.6 * math.exp(-0.3 * l) for l in range(DEPTH)]
        PT0 = sb("PT0", [128, DEPTH, 256], F32, at=PW0)
        PT1 = sb("PT1", [128, 256], F32, at=PW0 + DEPTH * 1024)
        PT2 = sb("PT2", [128, 8], F32, at=PW0 + DEPTH * 1024 + 1024)
        PTb = Buf()
        pw_bufs = [PTb]
        ACT(PT0[:], LBP[:], AF.Exp, [PRMb], [PTb])
        CP(PT1[:], PT0[:, 0, :], [PTb], [PTb])
        for l in range(1, DEPTH):
            TT(PT1[:], PT1[:], PT0[:, l, :], ALU.add, [PTb], [PTb])
        RECIP(PT1[:], PT1[:], [PTb], [PTb])
        for l in range(DEPTH):
            TT(PT0[:, l, :], PT0[:, l, :], PT1[:], ALU.mult, [PTb], [PTb])
        CP(LBB[:, 0, :], PT0[:, 0, :], [PTb], [PRMb])
        for l in range(1, DEPTH):
            TT(LBB[:, l, :], LBB[:, l - 1, :], PT0[:, l, :], ALU.add, [PTb, PRMb], [PRMb])
        for l in range(DEPTH):
            TT(LBB[:, l, :], LBB[:, l, :], PT0[:, 0, :], ALU.subtract, [PTb, PRMb], [PRMb])
            TS(OMLB[:, l, :], LBB[:, l, :], -1.0, ALU.mult, [PRMb], [PRMb], s2=1.0, op1=ALU.add)
        for l in range(DEPTH):
            TT(PT1[:, 0:64], LAMI[:, l, 0, :], LAMI[:, l, 1, :], ALU.mult, [PRMb, PTb], [PTb])
            TT(PT1[:, 64:128], LAMI[:, l, 2, :], LAMI[:, l, 3, :], ALU.mult, [PRMb, PTb], [PTb])
            RED(PT2[:, 0:2], PT1[:, 0:128].rearrange("p (a b) -> p a b", a=2), [PTb], [PTb])
            ACT(PT2[:, 0:2], PT2[:, 0:2], AF.Exp, [PTb], [PTb])
            TT(PT2[:, 2:3], PT2[:, 1:2], PT2[:, 0:1], ALU.subtract, [PTb], [PTb])
            TS(NLAM[:, l, 0:1], PT2[:, 2:3], -lam_init[l], ALU.add, [PTb, PRMb], [PRMb])
            TS(GQK[:, l, 0:2, :], GQK[:, l, 0:2, :], 0.125, ALU.mult, [PRMb], [PRMb])
            TS(SLG[:, l, :], SLG[:, l, :], 1.0 - lam_init[l], ALU.mult, [PRMb], [PRMb])

        def new_phase(n):
            bs = [Buf() for _ in range(n)]
            S.alias(bs, pw_bufs)
            pw_bufs[:] = bs
            return bs

        def norm_phase(GC, l):
            bs = new_phase(5)
            bj, bss, bxs0, bxs1, brs = bs
            JUNK = sb("JUNK", [128, D], BF16, at=PW0)
            XS = [sb("XS%d" % i, [128, D], BF16, at=PW0 + 2048 + 2048 * i) for i in range(2)]
            SSQ = sb("SSQ", [128, NT], F32, at=PW0 + 6144)
            RS = sb("RS", [128, NT], F32, at=PW0 + 6144 + 128)
            bxs = [bxs0, bxs1]
            for i in range(NT):
                ACT(JUNK[:], X[:, i, :], AF.Square, [Xb[i]], [bj, bss], accum=SSQ[:, i:i + 1])
            ACT(RS[:], SSQ[:], AF.Sqrt, [bss, CBb], [brs], bias=EPSt[:], scale=1.0 / D)
            RECIP(RS[:], RS[:], [brs], [brs])
            for i in range(NT):
                p = i % 2
                ACT(XS[p][:], X[:, i, :], AF.Copy, [Xb[i], brs], [bxs[p]], scale=RS[:, i:i + 1])
                pt = pbf(p)
                for k in range(8):
                    TR(pt[:, k * 128:(k + 1) * 128], XS[p][:, k * 128:(k + 1) * 128], [bxs[p]], [PSb[p]])
                TT(HT[:, :, i * 128:(i + 1) * 128], pt.rearrange("p (k t) -> p k t", k=8),
                   GC[:, l, :].unsqueeze(2).to_broadcast([128, 8, 128]), ALU.mult, [PSb[p], PRMb], [HTb[i]])

        def wsrc_in(l, c0, n):
            return w_in[l, 0:D, c0:c0 + n].rearrange("(k p) n -> p k n", p=128)

        def wview(ncols):
            return lambda t: t[:, 0:8 * ncols].rearrange("p (k n) -> p k n", k=8)

        def mlstm_phase(l):
            bs = new_phase(12)
            bqk = bs[0:4]
            bpre, bacc, bsg, bhalo = bs[4:6], bs[6:8], bs[8], bs[9]
            QKT = sb("QKT", [128, 4, T], BF16, at=PW0)
            o = PW0 + 8 * T
            PRE = [sb("PRE%d" % i, [128, 515], F32, at=o + 2080 * i) for i in range(2)]
            ACC = [sb("ACC%d" % i, [128, 512], F32, at=o + 4160 + 2048 * i) for i in range(2)]
            SG = sb("SGm", [128, 512], F32, at=o + 8256)
            HALO = sb("HALO", [128, 4, 3], F32, at=o + 10304)
            Wqk, bw = WLOAD([(wview(512), wsrc_in(l, 0, 512))])
            Wqv = wview(512)(Wqk)
            S.op("dve", lambda e: e.memset(HALO[:], 0.0), [], [bhalo])
            it = 0
            for G in range(NG):
                for ct in range(4):
                    p = it % 2
                    it += 1
                    pb_ = 6 + p
                    for k in range(8):
                        MM(PS[pb_][:], Wqv[:, k, ct * 128:(ct + 1) * 128], HT[:, k, G * 512:(G + 1) * 512],
                           k == 0, k == 7, [bw] + HTb[4 * G:4 * G + 4], [PSb[pb_]])
                    CP(PRE[p][:, 0:3], HALO[:, ct, :], [bhalo], [bpre[p]])
                    CP(PRE[p][:, 3:515], PS[pb_][:], [PSb[pb_]], [bpre[p]], eng="act")
                    CP(HALO[:, ct, :], PRE[p][:, 512:515], [bpre[p]], [bhalo])
                    TS(ACC[p][:], PRE[p][:, 0:512], CW[:, l, ct, 0:1], ALU.mult, [bpre[p], PRMb], [bacc[p]])
                    for j in range(1, 4):
                        STT(ACC[p][:], PRE[p][:, j:j + 512], CW[:, l, ct, j:j + 1], ACC[p][:], ALU.mult, ALU.add,
                            [bpre[p], PRMb, bacc[p]], [bacc[p]])
                    if ct < 2:
                        ACT(QKT[:, ct, G * 512:(G + 1) * 512], ACC[p][:], AF.Silu, [bacc[p]], [bqk[ct]])
                    else:
                        ACT(SG[:], ACC[p][:], AF.Sigmoid, [bacc[p]], [bsg])
                        STT(QKT[:, ct, G * 512:(G + 1) * 512], ACC[p][:], 0.125, SG[:], ALU.mult, ALU.mult,
                            [bacc[p], bsg], [bqk[ct]])
            if sub == 1:
                return
            old = [bpre[0], bpre[1], bacc[0], bacc[1], bsg, bhalo]
            nb = [Buf() for _ in range(16)]
            S.alias(nb, old)
            pw_bufs.extend(nb)
            bsm, bvbe, bkt, bsmt = nb[0:2], nb[2:4], nb[4:6], nb[6:8]
            bct, bctb, btc, bhh, bog, byb = nb[8], nb[9], nb[10], nb[11], nb[12], nb[13:15]
            SM = [sb("SMm%d" % i, [128, 64], F32, at=o + 256 * i) for i in range(2)]
            VBE = [sb("VBE%d" % i, [128, 4, 65], BF16, at=o + 512 + 544 * i) for i in range(2)]
            KTK = [sb("KTK%d" % i, [128, 256], BF16, at=o + 1600 + 512 * i) for i in range(2)]
            SMT = [sb("SMT%d" % i, [128, 4, 128], BF16, at=o + 2624 + 1024 * i) for i in range(2)]
            CTF = sb("CTF", [128, 2, 65], F32, at=o + 4672)
            CTB = sb("CTB", [128, 2, 65], BF16, at=o + 5216)
            TC = sb("TCm", [128, 2, 65], F32, at=o + 5504)
            HH = sb("HHm", [128, 256], F32, at=o + 6048)
            SQ = sb("SQm", [128, 256], F32, at=o + 7072)
            OG = sb("OGm", [128, 256], F32, at=o + 8096)
            YB = [sb("YBm%d" % i, [128, 256], BF16, at=o + 9120 + 512 * i) for i in range(2)]
            HLm = [sb("HLm%d" % i, [128, 8], BF16, at=o + 10144 + 32 * i) for i in range(2)]
            QZ = [sb("QZm%d" % i, [128, 4, 128], BF16, at=o + 10240 + 1024 * i) for i in range(2)]
            bqz = [Buf(), Buf()]
            S.alias(bqz, old)
            pw_bufs.extend(bqz)
            for i_ in range(2):
                S.op("dve", lambda e, i_=i_: e.memset(QZ[i_][:], 0.0), [], [bqz[i_]])
            bhl = [Buf(), Buf()]
            S.alias(bhl, old)
            pw_bufs.extend(bhl)
            Wvo, bwv = WLOAD([(wview(512), wsrc_in(l, 512, 512))])
            Wvv = wview(512)(Wvo)
            S.op("dve", lambda e: e.memset(CTF[:], 0.0), [], [bct])
            S.op("dve", lambda e: e.memset(CTB[:], 0.0), [], [bctb])
            OG2 = [OG, sb("OGm1", [128, 256], F32, at=o + 12288)]
            bog2 = [bog, Buf()]
            S.alias([bog2[1]], old)
            pw_bufs.append(bog2[1])
            mcaps = []
            for c in range(NT):
                p = c % 2
                cs = slice(c * 128, (c + 1) * 128)
                pvo = 0 if p == 0 else 5
                sm = SM[p]
                OG, bog = OG2[p], bog2[p]
                S.cap = mF = []
                for k in range(8):
                    MM(PS[pvo][:], HT[:, k, cs], Wvv[:, k, :], k == 0, k == 7, [HTb[c], bwv], [PSb[pvo]])
                ACT(OG[:], PS[pvo][:, 256:512], AF.Sigmoid, [PSb[pvo]], [bog])
                TT(OG[:].rearrange("p (h d) -> p h d", h=4), OG[:].rearrange("p (h d) -> p h d", h=4),
                   MNG[:, l, :].unsqueeze(1).to_broadcast([128, 4, 64]), ALU.mult, [bog, PRMb], [bog])
                for k in range(8):
                    MM(PS[1][:, 0:8], HT[:, k, cs], WG[:, l, k, :], k == 0, k == 7, [HTb[c], WGb], [PSb[1]])
                TT(sm[:, 0:8], PS[1][:, 0:8], GBB[:, l, :], ALU.add, [PSb[1], PRMb], [bsm[p]])
                ACT(sm[:, 8:12], sm[:, 4:8], AF.Exp, [bsm[p]], [bsm[p]], scale=-1.0)
                ACT(sm[:, 8:12], sm[:, 8:12], AF.Ln, [bsm[p], CBb], [bsm[p]], bias=ONEt[:])
                hl_ = HLm[p]
                CP(hl_[:, 0:4], sm[:, 8:12], [bsm[p]], [bhl[p]])
                TT(sm[:, 48:52], sm[:, 8:12], hl_[:, 0:4], ALU.subtract, [bsm[p], bhl[p]], [bsm[p]])
                CP(hl_[:, 4:8], sm[:, 48:52], [bsm[p]], [bhl[p]])
                for q_ in range(2):
                    MM(PS[1][:, 8:12], MSKB[:], hl_[:, 4 * q_:4 * q_ + 4], q_ == 0, q_ == 1, [CBb, bhl[p]], [PSb[1]])
                for q_ in range(2):
                    MM(PS[1][:, 12:16], ONESB[:], hl_[:, 4 * q_:4 * q_ + 4], q_ == 0, q_ == 1, [CBb, bhl[p]], [PSb[1]])
                TT(sm[:, 12:16], sm[:, 0:4], PS[1][:, 8:12], ALU.add, [bsm[p], PSb[1]], [bsm[p]])
                ACT(sm[:, 16:20], sm[:, 12:16], AF.Exp, [bsm[p]], [bsm[p]])
                ACT(sm[:, 20:28], PS[1][:, 8:16], AF.Exp, [PSb[1]], [bsm[p]], scale=-1.0)
                CP(sm[0:64, 28:30], sm[0:64, 24:28:2], [bsm[p]], [bsm[p]])
                CP(sm[64:128, 28:30], sm[64:128, 25:28:2], [bsm[p]], [bsm[p]])
                TT(VBE[p][:, :, 0:64], PS[pvo][:, 0:256].rearrange("p (h d) -> p h d", h=4),
                   sm[:, 16:20].unsqueeze(2).to_broadcast([128, 4, 64]), ALU.mult, [PSb[pvo], bsm[p]], [bvbe[p]])
                CP(VBE[p][:, :, 64:65], sm[:, 16:20].unsqueeze(2), [bsm[p]], [bvbe[p]])
                pk = pbf(2)
                for j in range(2):
                    TR(pk[:, j * 128:(j + 1) * 128], QKT[:, 2 + j, cs], [bqk[2 + j]], [PSb[2]])
                CP(KTK[p][:], pk[:, 0:256], [PSb[2]], [bkt[p]], eng="act")
                CP(QZ[p][0:64, 0:4:2, :], QKT[0:64, 0:2, cs], [bqk[0], bqk[1]], [bqz[p]])
                CP(QZ[p][64:128, 1:4:2, :], QKT[64:128, 0:2, cs], [bqk[0], bqk[1]], [bqz[p]])
                for h in range(4):
                    MM(PS[3][:, h * 128:(h + 1) * 128], QKT[:, 2 + h // 2, cs], QZ[p][:, h, :], True, True,
                       [bqk[2 + h // 2], bqz[p]], [PSb[3]])
                TT(SMT[p][:], PS[3][:].rearrange("p (h t) -> p h t", h=4),
                   MSKB[:].unsqueeze(1).to_broadcast([128, 4, 128]), ALU.mult, [PSb[3], CBb], [bsmt[p]])
                S.cap = mB = []
                pn = PS[4][:, 0:260].rearrange("p (h e) -> p h e", h=4)
                for h in range(4):
                    hs = slice((h % 2) * 64, (h % 2) * 64 + 64)
                    MM(pn[:, h, :], SMT[p][:, h, :], VBE[p][:, h, :], True, False, [bsmt[p], bvbe[p]], [PSb[4]])
                    MM(pn[:, h, :], QZ[p][:, h, :], CTB[:, h // 2, :], False, True, [bqz[p], bctb], [PSb[4]])
                pc = PS[7][:, 0:260].rearrange("p (a e) -> p a e", a=2)
                for hp in range(2):
                    MM(pc[:, hp, :], KTK[p][:, hp * 128:(hp + 1) * 128],
                       VBE[p][:, 2 * hp:2 * hp + 2, :].rearrange("p a e -> p (a e)"), True, True,
                       [bkt[p], bvbe[p]], [PSb[7]])
                ebl = sm[:, 28:30].unsqueeze(2).to_broadcast([128, 2, 65])
                for hl in range(2):
                    hs = slice(hl * 64, hl * 64 + 64)
                    TT(TC[hs, :, :], pc[hs, :, hl * 65:(hl + 1) * 65], sm[hs, 28:30].unsqueeze(2).to_broadcast([64, 2, 65]),
                       ALU.mult, [PSb[7], bsm[p]], [btc])
                TT(CTF[:], CTF[:], ebl, ALU.mult, [bct, bsm[p]], [bct])
                TT(CTF[:], CTF[:], TC[:], ALU.add, [bct, btc], [bct])
                TT(sm[:, 32:36], pn[:, :, 64:65].rearrange("p h e -> p (h e)"), sm[:, 20:24], ALU.mult, [PSb[4], bsm[p]], [bsm[p]])
                TS(sm[:, 44:48], sm[:, 32:36], -1.0, ALU.mult, [bsm[p]], [bsm[p]], s2=1.0, op1=ALU.max)
                TT(sm[:, 32:36], sm[:, 32:36], sm[:, 44:48], ALU.max, [bsm[p]], [bsm[p]])
                RECIP(sm[:, 32:36], sm[:, 32:36], [bsm[p]], [bsm[p]])
                TT(sm[:, 36:40], sm[:, 32:36], sm[:, 20:24], ALU.mult, [bsm[p]], [bsm[p]])
                TT(HH[:].rearrange("p (h d) -> p h d", h=4), pn[:, :, 0:64],
                   sm[:, 36:40].unsqueeze(2).to_broadcast([128, 4, 64]), ALU.mult, [PSb[4], bsm[p]], [bhh])
                CP(CTB[:], CTF[:], [bct], [bctb])
                ACT(SQ[:], HH[:], AF.Square, [bhh], [bhh])
                RED(sm[:, 40:44], SQ[:].rearrange("p (h d) -> p h d", h=4), [bhh], [bsm[p]])
                RSTD(sm[:, 40:44], sm[:, 40:44], 1.0 / 64, [bsm[p], CBb], [bsm[p]])
                TT(HH[:].rearrange("p (h d) -> p h d", h=4), HH[:].rearrange("p (h d) -> p h d", h=4),
                   sm[:, 40:44].unsqueeze(2).to_broadcast([128, 4, 64]), ALU.mult, [bhh, bsm[p]], [bhh])
                TT(YB[p][:], HH[:], OG[:], ALU.mult, [bhh, bog], [byb[p]])
                py = pbf(3 if False else 6)
                for j in range(2):
                    TR(py[:, j * 128:(j + 1) * 128], YB[p][:, j * 128:(j + 1) * 128], [byb[p]], [PSb[6]])
                CP(MIXT[:, 0:2, cs], py[:, 0:256].rearrange("p (j t) -> p j t", j=2), [PSb[6]], [MXb[c]], eng="act")
                S.cap = None
                mcaps.append((mF, mB))
            for it_ in mcaps[0][0]:
                S.op(*it_)
            for c in range(NT):
                F_ = mcaps[c + 1][0] if c + 1 < NT else []
                B_ = mcaps[c][1]
                for j in range(max(len(F_), len(B_))):
                    if j < len(B_):
                        S.op(*B_[j])
                    if j < len(F_):
                        S.op(*F_[j])

        class Pipe:
            def __init__(self):
                self.q, self.t = {}, 0

            def at(self, d, fn, tag=None):
                if FLAT:
                    d = 0
                self.q.setdefault(self.t + d, []).append((fn, tag))

            def tick(self):
                for fn, _ in self.q.pop(self.t, []):
                    fn()
                self.t += 1

            def force(self, pred):
                for t in sorted(self.q):
                    keep = []
                    for fn, tag in self.q[t]:
                        if tag is not None and pred(tag):
                            fn()
                        else:
                            keep.append((fn, tag))
                    self.q[t] = keep
                for t in [t for t in self.q if not self.q[t]]:
                    del self.q[t]

            def flush(self):
                while self.q:
                    self.tick()

        def attn_phase(l):
            KT = sb("AKT", [128, T], BF16, at=PW0)
            QZ = sb("AQZ", [128, 2, T], BF16, at=PW0 + 2 * T)
            VX = sb("AVX", [128, NT, 129], BF16, at=PW0 + 6 * T)
            for h in range(4):
                bs = new_phase(50)
                it_ = iter(bs)
                nb = lambda n: [next(it_) for _ in range(n)]
                bkt, bqz0, bqz1, bvx, bjk = nb(5)
                bsq, bss, bqr, bt1, bt2, bqb = nb(2), nb(4), nb(6), nb(2), nb(2), nb(3)
                bpt, bsml, bto, boo, bya = nb(6), nb(3), nb(2), nb(2), nb(2)
                o = PW0 + 6 * T
                o += (NT * 258 + 31) // 32 * 32
                SQ = [sb("ASQ%d" % i, [128, 256], F32, at=o + 1024 * i) for i in range(2)]
                o += 2048
                QR = [sb("AQR%d" % i, [128, 4, 64], F32, at=o + 1024 * i) for i in range(6)]
                o += 6144
                T1 = [sb("AT1%d" % i, [128, 4, 16], F32, at=o + 256 * i) for i in range(2)]
                o += 512
                T2 = [sb("AT2%d" % i, [128, 4, 16], F32, at=o + 256 * i) for i in range(2)]
                o += 512
                SS = [sb("ASS%d" % i, [128, 8], F32, at=o + 32 * i) for i in range(4)]
                o += 128
                QB = [sb("AQB%d" % i, [128, 256], BF16, at=o + 512 * i) for i in range(3)]
                o += 1536
                PT = [sb("APT%d" % i, [128, 512], BF16, at=o + 1024 * i) for i in range(6)]
                o += 6144
                SML = [sb("ASM%d" % i, [128, 8], F32, at=o + 32 * i) for i in range(3)]
                o += 96
                PO = [sb("APO%d" % i, [128, 2, 129], F32, at=o + 1056 * i) for i in range(2)]
                o += 2112
                OO = [sb("AOO%d" % i, [128, 128], F32, at=o + 512 * i) for i in range(2)]
                o += 1024
                YA = [sb("AYA%d" % i, [128, 128], BF16, at=o + 256 * i) for i in range(2)]
                o += 512
                JK = sb("AJK", [128, 128], F32, at=o)
                o += 512
                assert o <= PWEND, (o, PWEND)
                if h == 0:
                    S.op("dve", lambda e, QZ=QZ: e.memset(QZ[:], 0.0), [], [bqz0, bqz1])
                    S.op("dve", lambda e, VX=VX: e.memset(VX[:, :, 128:129], 1.0), [], [bvx])
                Wa, bwa = WLOAD([(lambda t, j=j: t[:, 0:3072].rearrange("p (k n) -> p k n", k=8)[:, :, j * 128:(j + 1) * 128],
                                  wsrc_in(l, 1032 + 512 * j + h * 128, 128)) for j in range(3)])
                Wav = Wa[:, 0:3072].rearrange("p (k n) -> p k n", k=8)
                pp = Pipe()
                for i in range(NT):
                    ts_ = slice(i * 128, (i + 1) * 128)
                    pa = i % 2
                    qr, sq, ss, t1, t2, qbt = QR[i % 6], SQ[i % 2], SS[i % 4], T1[i % 2], T2[i % 2], QB[i % 3]
                    b_qr, b_sq, b_ss, b_t1, b_t2, b_qb = bqr[i % 6], bsq[i % 2], bss[i % 4], bt1[i % 2], bt2[i % 2], bqb[i % 3]
                    ptb = 2 + i % 2
                    qb3 = qbt[:].rearrange("p (a d) -> p a d", a=4)

                    def s0(i=i, ts_=ts_, pa=pa):
                        for k in range(8):
                            MM(PS[pa][:, 0:384], HT[:, k, ts_], Wav[:, k, :], k == 0, k == 7, [HTb[i], bwa], [PSb[pa]])

                    def s1(i=i, pa=pa, qr=qr, b_qr=b_qr):
                        CP(VX[:, i, 0:128], PS[pa][:, 256:384], [PSb[pa]], [bvx], eng="act")
                        CP(qr[:].rearrange("p a d -> p (a d)"), PS[pa][:, 0:256], [PSb[pa]], [b_qr], eng="act")

                    def s1b(sq=sq, qr=qr, b_sq=b_sq, b_qr=b_qr):
                        ACT(sq[:], qr[:].rearrange("p a d -> p (a d)"), AF.Square, [b_qr], [b_sq])

                    def s2(sq=sq, ss=ss, b_sq=b_sq, b_ss=b_ss):
                        RED(ss[:, 0:4], sq[:].rearrange("p (a d) -> p a d", a=4), [b_sq], [b_ss])

                    def s3(ss=ss, b_ss=b_ss):
                        ACT(ss[:, 0:4], ss[:, 0:4], AF.Sqrt, [b_ss, CBb], [b_ss], bias=EPSt[:], scale=1.0 / 64)

                    def s4(i=i, ss=ss, qr=qr, t1=t1, t2=t2, qb3=qb3, b_ss=b_ss, b_qr=b_qr, b_t1=b_t1, b_t2=b_t2, b_qb=b_qb):
                        RECIP(ss[:, 0:4], ss[:, 0:4], [b_ss], [b_ss])
                        TT(qr[:], qr[:], ss[:, 0:4].unsqueeze(2).to_broadcast([128, 4, 64]), ALU.mult, [b_qr, b_ss], [b_qr])
                        TT(qr[:], qr[:], GQK[:, l, :, :], ALU.mult, [b_qr, PRMb], [b_qr])
                        TT(t1[:], qr[:, :, 0:16], ROPE[:, i, 0:16].unsqueeze(1).to_broadcast([128, 4, 16]), ALU.mult,
                           [b_qr, ROPEb], [b_t1])
                        TT(t2[:, :, 0:8], qr[:, :, 8:16], ROPE[:, i, 16:24].unsqueeze(1).to_broadcast([128, 4, 8]), ALU.mult,
                           [b_qr, ROPEb], [b_t2])
                        TT(t2[:, :, 8:16], qr[:, :, 0:8], ROPE[:, i, 24:32].unsqueeze(1).to_broadcast([128, 4, 8]), ALU.mult,
                           [b_qr, ROPEb], [b_t2])
                        TT(qb3[:, :, 0:16], t1[:], t2[:], ALU.add, [b_t1, b_t2], [b_qb])

                    def s5(qr=qr, qb3=qb3, b_qr=b_qr, b_qb=b_qb):
                        CP(qb3[:, :, 16:64], qr[:, :, 16:64], [b_qr], [b_qb], eng="act")

                    def s6(qbt=qbt, b_qb=b_qb, ptb=ptb):
                        pt = pbf(ptb)
                        for j in range(2):
                            TR(pt[:, j * 128:(j + 1) * 128], qbt[:, j * 128:(j + 1) * 128], [b_qb], [PSb[ptb]])

                    def s7(ts_=ts_, ptb=ptb):
                        pt = pbf(ptb)
                        CP(KT[:, ts_], pt[:, 128:256], [PSb[ptb]], [bkt], eng="act")
                        CP(QZ[0:64, 0, ts_], pt[0:64, 0:128], [PSb[ptb]], [bqz0], eng="act")
                        CP(QZ[64:128, 1, ts_], pt[64:128, 0:128], [PSb[ptb]], [bqz1], eng="act")

                    for d_, fn_ in enumerate((s0, s1, s1b, s2, s3, s4, s5, s6, s7)):
                        if 30 <= sub < 40 and d_ > sub - 30:
                            continue
                        pp.at(d_, fn_)
                    pp.tick()
                pp.flush()
                if sub == 3 or 30 <= sub < 40:
                    continue
                groups = []
                for qb in range(NT):
                    ng = (qb + 2) // 2
                    for g in range(ng):
                        groups.append((qb, [kb for kb in (2 * g, 2 * g + 1) if kb <= qb], g == ng - 1))
                pp = Pipe()
                for n, (qb, kbs, last) in enumerate(groups):
                    sbk, pbk, p = n % 3, n % 6, qb % 2
                    qs_ = slice(qb * 128, (qb + 1) * 128)
                    nj = len(kbs)

                    def gS(kbs=kbs, sbk=sbk, qs_=qs_):
                        for j, kb in enumerate(kbs):
                            MM(PS[sbk][:, j * 256:(j + 1) * 256].rearrange("p (c t) -> p c t", c=2),
                               KT[:, kb * 128:(kb + 1) * 128], QZ[:, :, qs_], True, True, [bkt, bqz0, bqz1], [PSb[sbk]])

                    def gE(sbk=sbk, pbk=pbk, nj=nj):
                        ACT(PT[pbk][:, 0:nj * 256], PS[sbk][:, 0:nj * 256], AF.Exp, [PSb[sbk]], [bpt[pbk]])

                    def gM(pbk=pbk, nj=nj):
                        ptv = PT[pbk][:, (nj - 1) * 256:nj * 256].rearrange("p (c t) -> p c t", c=2)
                        TT(ptv, ptv, MSKB[:].unsqueeze(1).to_broadcast([128, 2, 128]), ALU.mult, [bpt[pbk], CBb], [bpt[pbk]])

                    def gV(kbs=kbs, pbk=pbk, p=p, qb=qb):
                        if kbs[0] == 0:
                            pp.force(lambda tg: tg <= qb - 2)
                        for c in range(2):
                            bk = 3 + 2 * p + c
                            for j, kb in enumerate(kbs):
                                MM(PS[bk][:, 0:129], PT[pbk][:, j * 256 + c * 128:j * 256 + (c + 1) * 128], VX[:, kb, :],
                                   kb == 0, kb == qb, [bpt[pbk], bvx], [PSb[bk]])

                    pp.at(0, gS)
                    pp.at(2, gE)
                    if last:
                        pp.at(4, gM)
                    pp.at(7, gV)
                    if last:
                        b0, b1 = 3 + 2 * p, 4 + 2 * p
                        sml, b_sml = SML[qb % 3], bsml[qb % 3]

                        def e1(b0=b0, b1=b1, p=p, qb=qb):
                            pp.force(lambda tg: tg <= qb - 2)
                            CP(PO[p][:, 0, :], PS[b0][:, 0:129], [PSb[b0]], [bto[p]], eng="act")
                            CP(PO[p][:, 1, :], PS[b1][:, 0:129], [PSb[b1]], [bto[p]], eng="act")

                        def e2(p=p, sml=sml, b_sml=b_sml):
                            RECIP(sml[:, 0:2], PO[p][:, :, 128:129].rearrange("p c e -> p (c e)"), [bto[p]], [b_sml])
                            TT(sml[:, 2:3], sml[:, 1:2], NLAM[:, l, 0:1], ALU.mult, [b_sml, PRMb], [b_sml])
                            TS(PO[p][:, 1, 0:128], PO[p][:, 1, 0:128], sml[:, 2:3], ALU.mult, [bto[p], b_sml], [bto[p]])

                        def e3(p=p, sml=sml, b_sml=b_sml):
                            STT(OO[p][:], PO[p][:, 0, 0:128], sml[:, 0:1], PO[p][:, 1, 0:128], ALU.mult, ALU.add,
                                [bto[p], b_sml], [boo[p]])

                        def e4(p=p, sml=sml, b_sml=b_sml):
                            ACT(JK[:], OO[p][:], AF.Square, [boo[p]], [bjk, b_sml], accum=sml[:, 3:4])
                            ACT(sml[:, 3:4], sml[:, 3:4], AF.Sqrt, [b_sml, CBb], [b_sml], bias=EPSt[:], scale=1.0 / 128)

                        def e5(p=p, sml=sml, b_sml=b_sml):
                            RECIP(sml[:, 3:4], sml[:, 3:4], [b_sml], [b_sml])
                            STT(YA[p][:], OO[p][:], sml[:, 3:4], SLG[:, l, :], ALU.mult, ALU.mult, [boo[p], b_sml, PRMb], [bya[p]])

                        def e6(p=p):
                            TR(pbf(7)[:, p * 128:(p + 1) * 128], YA[p][:], [bya[p]], [PSb[7]])

                        def e7(qb=qb, qs_=qs_, p=p):
                            CP(MIXT[:, 2 + h, qs_], pbf(7)[:, p * 128:(p + 1) * 128], [PSb[7]], [MXb[qb]], eng="act")

                        for d_, fn_ in ((9, e1), (11, e2), (13, e3), (15, e4), (18, e5), (20, e6), (22, e7)):
                            pp.at(d_, fn_, tag=qb)
                    pp.tick()
                pp.flush()

        def hgrn_phase(l):
            bs = new_phase(26)
            (bqs, bsg, bf, blf, bkk, beq, bek, besc, bsf, _u, bts, bamt, bsq, bgg, bhh, bss) = bs[0:16]
            bqh, bvh, bqkt, byb, bsb = bs[16:18], bs[18:20], bs[20:22], bs[22:24], bs[24:26]
            o = PW0
            f32t = lambda name, i: sb(name, [128, 256], F32, at=o + 1024 * i)
            QS, SGt, Ft, LF, KK, EQ, EK, SQ, GG, HHt = [f32t("H%d" % i, i) for i in range(10)]
            o2 = o + 10240
            QH = [sb("HQH%d" % i, [128, 512], BF16, at=o2 + 1024 * i) for i in range(2)]
            VH = [sb("HVH%d" % i, [128, 256], BF16, at=o2 + 2048 + 512 * i) for i in range(2)]
            QKHT = [sb("HQKT%d" % i, [64, 8, 128], BF16, at=o2 + 3072 + 2048 * i) for i in range(2)]
            ESC = sb("HESC", [64, 4, 8], F32, at=o2 + 7168)
            SF = sb("HSF", [64, 4, 64], F32, at=o2 + 7296)
            SBa = [sb("HSB%d" % i, [64, 4, 64], BF16, at=o2 + 8320 + 11936 - 8320 + 512 * i) for i in range(2)]
            TS_ = sb("HTS", [64, 4, 64], F32, at=o2 + 8832)
            AMT = sb("HAMT", [128, 4, 128], BF16, at=o2 + 9856)
            SSm = sb("HSS", [128, 8], F32, at=o2 + 10880)
            YB = [sb("HYB%d" % i, [128, 256], BF16, at=o2 + 10912 + 512 * i) for i in range(2)]
            LFH = sb("HLFH", [128, 512], BF16, at=o2 + 12960)
            blfh = Buf()
            S.alias([blfh], bs)
            pw_bufs.append(blfh)
            GG2 = [GG, sb("HGG1", [128, 256], F32, at=o2 + 13984)]
            AMT2 = [AMT, sb("HAMT1", [128, 4, 128], BF16, at=o2 + 15008)]
            ESC2 = [ESC, sb("HESC1", [64, 4, 8], F32, at=o2 + 16032)]
            xb = [Buf(), Buf(), Buf()]
            S.alias(xb, bs)
            pw_bufs.extend(xb)
            bgg2, bamt2, besc2 = [bgg, xb[0]], [bamt, xb[1]], [besc, xb[2]]
            caps = []
            W1, bw1 = WLOAD([(wview(512), wsrc_in(l, 2568, 512))])
            W2, bw2 = WLOAD([(wview(512), wsrc_in(l, 3080, 512))])
            W1v, W2v = wview(512)(W1), wview(512)(W2)
            S.op("dve", lambda e: e.memset(SF[:], 0.0), [], [bsf])
            for i in range(NT):
                p = i % 2
                ts_ = slice(i * 128, (i + 1) * 128)
                GG, AMT, ESC = GG2[p], AMT2[p], ESC2[p]
                bgg, bamt, besc = bgg2[p], bamt2[p], besc2[p]
                S.cap = capF = []
                for k in range(8):
                    MM(PS[0][:], HT[:, k, ts_], W1v[:, k, :], k == 0, k == 7, [HTb[i], bw1], [PSb[0]])
                for k in range(8):
                    MM(PS[1][:], HT[:, k, ts_], W2v[:, k, :], k == 0, k == 7, [HTb[i], bw2], [PSb[1]])
                ACT(QS[:], PS[0][:, 0:256], AF.Silu, [PSb[0]], [bqs])
                ACT(SGt[:], PS[0][:, 256:512], AF.Sigmoid, [PSb[0]], [bsg])
                TT(Ft[:], SGt[:], OMLB[:, l, :], ALU.mult, [bsg, PRMb], [bf])
                TT(Ft[:], Ft[:], LBB[:, l, :], ALU.add, [bf, PRMb], [bf])
                ACT(LF[:], Ft[:], AF.Ln, [bf], [blf])
                TS(KK[:], Ft[:], -1.0, ALU.mult, [bf], [bkk], s2=1.0, op1=ALU.add)
                CP(VH[p][:], PS[1][:, 0:256], [PSb[1]], [bvh[p]], eng="act")
                ACT(GG[:], PS[1][:, 256:512], AF.Silu, [PSb[1]], [bgg])
                TT(GG[:].rearrange("p (h d) -> p h d", h=4), GG[:].rearrange("p (h d) -> p h d", h=4),
                   HNG[:, l, :].unsqueeze(1).to_broadcast([128, 4, 64]), ALU.mult, [bgg, PRMb], [bgg])
                CP(LFH[:, 0:256], LF[:], [blf], [blfh])
                TT(EQ[:], LF[:], LFH[:, 0:256], ALU.subtract, [blf, blfh, beq], [beq])
                CP(LFH[:, 256:512], EQ[:], [beq], [blfh])
                for q_ in range(2):
                    MM(PS[2][:, 0:256], TSHB[:], LFH[:, 256 * q_:256 * q_ + 256], q_ == 0, q_ == 1, [CBb, blfh], [PSb[2]])
                psc = PS[3][0:64, 0:16].rearrange("p (h e) -> p h e", h=4)
                for hh in range(4):
                    for q_ in range(2):
                        MM(psc[:, hh, :], LFH[:, 256 * q_ + hh * 64:256 * q_ + (hh + 1) * 64], SELB[:], q_ == 0, q_ == 1,
                           [blfh, CBb], [PSb[3]])
                ACT(EQ[:], PS[2][:, 0:256], AF.Exp, [PSb[2]], [beq])
                ACT(EK[:], PS[2][:, 0:256], AF.Exp, [PSb[2]], [bek], scale=-1.0)
                ACT(ESC[:, :, 0:4], psc, AF.Exp, [PSb[3]], [besc])
                TT(ESC[:, :, 4:6], ESC[:, :, 0:4:2], ESC[:, :, 1:4:2], ALU.mult, [besc], [besc])
                TT(QH[p][:, 0:256], QS[:], EQ[:], ALU.mult, [bqs, beq], [bqh[p]])
                TT(QH[p][:, 256:512], KK[:], EK[:], ALU.mult, [bkk, bek], [bqh[p]])
                pt = pbf(4)[0:64, :]
                for j in range(8):
                    TR(pt[:, j * 128:(j + 1) * 128], QH[p][:, j * 64:(j + 1) * 64], [bqh[p]], [PSb[4]])
                CP(QKHT[p][:], pt.rearrange("p (j t) -> p j t", j=8), [PSb[4]], [bqkt[p]])
                for hh in range(4):
                    MM(PS[5][:, hh * 128:(hh + 1) * 128], QKHT[p][:, 4 + hh, :], QKHT[p][:, hh, :], True, True,
                       [bqkt[p]], [PSb[5]])
                TT(AMT[:], PS[5][:].rearrange("p (h t) -> p h t", h=4), MSKHB[:].unsqueeze(1).to_broadcast([128, 4, 128]),
                   ALU.mult, [PSb[5], CBb], [bamt])
                S.cap = capB = []
                po = PS[6][:, 0:256].rearrange("p (h d) -> p h d", h=4)
                for a in range(2):
                    asl = slice(a * 64, a * 64 + 64)
                    TT(SBa[a][:], SF[:], ESC[:, :, 2 * a:2 * a + 1].to_broadcast([64, 4, 64]), ALU.mult, [bsf, besc], [bsb[a]])
                    ps2 = PS[7][0:64, 0:256].rearrange("p (h d) -> p h d", h=4)
                    for hh in range(4):
                        MM(ps2[:, hh, :], QH[p][asl, 256 + hh * 64:256 + (hh + 1) * 64], VH[p][asl, hh * 64:(hh + 1) * 64],
                           True, True, [bqh[p], bvh[p]], [PSb[7]])
                    TT(TS_[:], ps2, ESC[:, :, 2 * a + 1:2 * a + 2].to_broadcast([64, 4, 64]), ALU.mult, [PSb[7], besc], [bts])
                    TT(SF[:], SF[:], ESC[:, :, 4 + a:5 + a].to_broadcast([64, 4, 64]), ALU.mult, [bsf, besc], [bsf])
                    TT(SF[:], SF[:], TS_[:], ALU.add, [bsf, bts], [bsf])
                for hh in range(4):
                    MM(po[:, hh, :], AMT[:, hh, :], VH[p][:, hh * 64:(hh + 1) * 64], True, False, [bamt, bvh[p]], [PSb[6]])
                    for a in range(2):
                        asl = slice(a * 64, a * 64 + 64)
                        MM(po[asl, hh, :], QKHT[p][:, hh, asl], SBa[a][:, hh, :], False, True, [bqkt[p], bsb[a]], [PSb[6]])
                ACT(SQ[:], PS[6][:, 0:256], AF.Square, [PSb[6]], [bsq])
                RED(SSm[:, 0:4], SQ[:].rearrange("p (h d) -> p h d", h=4), [bsq], [bss])
                RSTD(SSm[:, 0:4], SSm[:, 0:4], 1.0 / 64, [bss, CBb], [bss])
                TT(HHt[:].rearrange("p (h d) -> p h d", h=4), po, SSm[:, 0:4].unsqueeze(2).to_broadcast([128, 4, 64]),
                   ALU.mult, [PSb[6], bss], [bhh])
                TT(YB[p][:], HHt[:], GG[:], ALU.mult, [bhh, bgg], [byb[p]])
                py = pbf(6)
                for j in range(2):
                    TR(py[:, 512 + j * 128:512 + (j + 1) * 128], YB[p][:, j * 128:(j + 1) * 128], [byb[p]], [PSb[6]])
                CP(MIXT[:, 6:8, ts_], py[:, 512:768].rearrange("p (j t) -> p j t", j=2), [PSb[6]], [MXb[i]], eng="act")
                S.cap = None
                caps.append((capF, capB))
            for it_ in caps[0][0]:
                S.op(*it_)
            for i in range(NT):
                F_ = caps[i + 1][0] if i + 1 < NT else []
                B_ = caps[i][1]
                for j in range(max(len(F_), len(B_))):
                    if j < len(B_):
                        S.op(*B_[j])
                    if j < len(F_):
                        S.op(*F_[j])

        def outproj_phase(l):
            ws = []
            for nh in range(2):
                ws.append(WLOAD([(wview(512), w_out[l, 0:D, nh * 512:(nh + 1) * 512].rearrange("(k p) n -> p k n", p=128))]))
            it = 0
            for i in range(NT):
                for nh in range(2):
                    b_ = it % 4
                    it += 1
                    wv_ = wview(512)(ws[nh][0])
                    for k in range(8):
                        MM(PS[b_][:], MIXT[:, k, i * 128:(i + 1) * 128], wv_[:, k, :], k == 0, k == 7,
                           [MXb[i], ws[nh][1]], [PSb[b_]])
                    TT(X[:, i, nh * 512:(nh + 1) * 512], PS[b_][:], X[:, i, nh * 512:(nh + 1) * 512], ALU.add,
                       [PSb[b_], Xb[i]], [Xb[i]])

        def ffn_phase(l):
            bs = new_phase(2)
            br = bs
            R = [sb("FR%d" % i, [128, 512], F32, at=PW0 + 8192 + 2048 * i) for i in range(2)]
            if T >= 2048:
                AT = MIXT[:].rearrange("p k t -> p (k t)")
                ATv = AT[:, 0:32 * 512].rearrange("p (j t) -> p j t", j=32)
            else:
                ATv = sb("ATs", [128, 32, 512], BF16, at=PW0 + 36000)[:]
            bat = [Buf() for _ in range(32)]
            it = 0
            for G in range(NG):
                S.alias(bat, MXb + bat)
                for jg in range(8):
                    wu, bwu = WLOAD([(wview(512), w_up[l, 0:D, jg * 512:(jg + 1) * 512].rearrange("(k p) n -> p k n", p=128))])
                    wuv = wview(512)(wu)
                    for jj in range(4):
                        j = jg * 4 + jj
                        p = it % 2
                        it += 1
                        for k in range(8):
                            MM(PS[p][:], wuv[:, k, jj * 128:(jj + 1) * 128], HT[:, k, G * 512:(G + 1) * 512], k == 0, k == 7,
                               [bwu] + HTb[4 * G:4 * G + 4], [PSb[p]])
                        ACT(R[p][:], PS[p][:], AF.Relu, [PSb[p]], [br[p]])
                        TT(ATv[:, j, :], R[p][:], R[p][:], ALU.mult, [br[p]], [bat[j]])
                for nh in range(2):
                    for jg in range(4):
                        wd, bwd = WLOAD([(wview(512), w_dn[l, jg * 1024:(jg + 1) * 1024, nh * 512:(nh + 1) * 512]
                                          .rearrange("(j p) n -> p j n", p=128))])
                        wdv = wview(512)(wd)
                        for ti in range(4):
                            for jj in range(8):
                                j = jg * 8 + jj
                                MM(PS[2 + ti][:], ATv[:, j, ti * 128:(ti + 1) * 128], wdv[:, jj, :], j == 0, j == 31,
                                   [bat[j], bwd], [PSb[2 + ti]])
                    for ti in range(4):
                        i = G * 4 + ti
                        TT(X[:, i, nh * 512:(nh + 1) * 512], PS[2 + ti][:], X[:, i, nh * 512:(nh + 1) * 512], ALU.add,
                           [PSb[2 + ti], Xb[i]], [Xb[i]])
            S.alias(MXb, bat)

        for s in range(NSEQ):
            for i in range(NT):
                S.dma("sp", lambda e, s=s, i=i: e.dma_start(out=X[:, i, :], in_=x_d[s * T + i * 128:s * T + (i + 1) * 128, :]),
                      "ldx%d" % i, writes=[Xb[i]])
            for l in range(DEPTH):
                if stop >= 1:
                    norm_phase(G1C, l)
                if stop >= 2:
                    mlstm_phase(l)
                if stop >= 3:
                    attn_phase(l)
                if stop >= 4:
                    hgrn_phase(l)
                if stop < 5:
                    continue
                if dbg and s == 0 and l == 0:
                    S.dma("sp", lambda e: e.dma_start(out=dbg_d[:, :, :], in_=MIXT[:]), "std", reads=MXb)
                outproj_phase(l)
                norm_phase(G2C, l)
                ffn_phase(l)
            for i in range(NT):
                S.dma("sp", lambda e, s=s, i=i: e.dma_start(out=y_d[s * T + i * 128:s * T + (i + 1) * 128, :], in_=X[:, i, :]),
                      "sty%d" % i, reads=[Xb[i]])
        S.wait_all("sp", Xb + MXb)
        global _LAST_S
        _LAST_S = S
        S.emit()
    return nc


_CACHE = {}


def kernel(**inputs):
    x = np.ascontiguousarray(inputs["x"], dtype=np.float32)
    B, T, _ = x.shape
    NCORE = 8
    NSEQ = B // NCORE
    DEPTH = inputs["w_in"].shape[0]
    key = (T, NSEQ, DEPTH)
    if key not in _CACHE:
        _CACHE[key] = build(T, NSEQ, DEPTH)
    nc = _CACHE[key]
    cst, rope = host_consts(T)
    shared = {k: np.ascontiguousarray(v, dtype=np.float32) for k, v in inputs.items() if k != "x"}
    shared["cst"] = cst
    shared["rope"] = rope
    in_maps = []
    for c in range(NCORE):
        m = dict(shared)
        for k in ("w_in", "w_out", "w_up", "w_down"):
            w = shared[k]
            wp = np.empty((w.shape[0], w.shape[1] + 1, w.shape[2]), np.float32)
            wp[:, :-1] = w
            wp[:, -1] = float(c)
            m[k] = wp
        m["x"] = x[c * NSEQ:(c + 1) * NSEQ].reshape(NSEQ * T, D)
        in_maps.append(m)
    res = run_bass_kernel_spmd(nc, in_maps, core_ids=list(range(NCORE)))
    out = np.concatenate([r["y"].reshape(NSEQ, T, D) for r in res.results], axis=0)
    return out.astype(np.float32)
```
